# Optimizing a Trainium2 kernel written in Bass

```python
import math
import jax, jax.numpy as jnp
from jax import lax
import numpy as np

D_MODEL = 1024
BATCH = 4
SEQ = 4096
DEPTH = 1

MEM_LEN = 256
CHUNK = 128
G_HEADS = 4
G_WIDTH = D_MODEL // 4
G_DIM = G_WIDTH // G_HEADS
DA_HEADS = 4
DA_WIDTH = D_MODEL // 2
DA_V_DIM = DA_WIDTH // DA_HEADS
DA_HEAD_DIM = DA_V_DIM // 2
DA_QK = DA_HEADS * 2 * DA_HEAD_DIM
M_HEADS = 4
M_WIDTH = D_MODEL // 4
M_HEAD_DIM = M_WIDTH // M_HEADS
MIX_WIDTH = G_WIDTH + DA_WIDTH + M_WIDTH
IN_WIDTH = 2 * G_WIDTH + 2 * DA_QK + DA_WIDTH + M_WIDTH
SPLITS = (G_WIDTH, 2 * G_WIDTH, 2 * G_WIDTH + DA_QK, 2 * G_WIDTH + 2 * DA_QK,
          2 * G_WIDTH + 2 * DA_QK + DA_WIDTH)
D_FF = 2816
CONV_W = 3
Q_BLOCK = 128
LN_EPS = 1e-5
ALPHA = (2.0 * DEPTH) ** 0.25
BETA = (8.0 * DEPTH) ** -0.25

kernel_name = "hymba_style_gmlp_diffattn_mem_deepnorm_encoder"


def layer_norm(x, g, b):
    xf = x.astype(jnp.float32)
    mu = jnp.mean(xf, axis=-1, keepdims=True)
    var = jnp.mean(jnp.square(xf - mu), axis=-1, keepdims=True)
    return ((xf - mu) * lax.rsqrt(var + LN_EPS)).astype(x.dtype) * g + b


def rms_norm(x, g):
    xf = x.astype(jnp.float32)
    ms = jnp.mean(jnp.square(xf), axis=-1, keepdims=True)
    return (xf * lax.rsqrt(ms + LN_EPS)).astype(x.dtype) * g


def alibi_slopes(n):
    return np.array([2.0 ** (-8.0 * (i + 1) / n) for i in range(n)], dtype=np.float32)


def gmlp_group(u, v, ws, bs, ln_g, ln_b):
    B, T, _ = u.shape
    u = jax.nn.gelu(u, approximate=False)
    v = layer_norm(jax.nn.gelu(v, approximate=False), ln_g, ln_b)
    vc = v.reshape(B, T // CHUNK, CHUNK, G_HEADS, G_DIM)
    s = jnp.einsum('gts,bcsgd->bctgd', ws, vc) + bs.T[None, None, :, :, None]
    return u * s.reshape(B, T, G_WIDTH)


def diff_attention(q, k, v, lam, slopes):
    B, T, H, _, dk = q.shape
    dv = v.shape[-1]
    nb = T // Q_BLOCK
    qb = (q * (dk ** -0.5)).reshape(B, nb, Q_BLOCK, H, 2, dk).transpose(1, 0, 2, 3, 4, 5)
    kpos = jnp.arange(T, dtype=jnp.float32)

    def block(args):
        qi, i = args
        s = jnp.einsum('bqhmd,bkhmd->bhmqk', qi, k).astype(jnp.float32)
        qpos = (i * Q_BLOCK + jnp.arange(Q_BLOCK)).astype(jnp.float32)
        dist = jnp.abs(qpos[:, None] - kpos[None, :])
        s = s - slopes[None, :, None, None, None] * dist
        p = jax.nn.softmax(s, axis=-1)
        a = p[:, :, 0] - lam * p[:, :, 1]
        return jnp.einsum('bhqk,bkhd->bqhd', a.astype(v.dtype), v)

    out = lax.map(block, (qb, jnp.arange(nb)))
    return out.transpose(1, 0, 2, 3, 4).reshape(B, T, H, dv)


def memory_attention(qm, kv):
    B, M, _ = kv.shape
    km, vm = jnp.split(kv, 2, axis=-1)
    km = km.reshape(B, M, M_HEADS, M_HEAD_DIM)
    vm = vm.reshape(B, M, M_HEADS, M_HEAD_DIM)
    s = jnp.einsum('bthd,bmhd->bhtm', qm * (M_HEAD_DIM ** -0.5), km).astype(jnp.float32)
    p = jax.nn.softmax(s, axis=-1)
    o = jnp.einsum('bhtm,bmhd->bthd', p.astype(vm.dtype), vm)
    return o.reshape(qm.shape[0], qm.shape[1], M_WIDTH)


def conv_ffn(h, w_up, conv_w, conv_b, w_down):
    a = h @ w_up
    c = a.shape[-1]
    a = lax.conv_general_dilated(
        a, conv_w[:, None, :].astype(a.dtype), window_strides=(1,),
        padding=((CONV_W // 2, CONV_W // 2),),
        dimension_numbers=('NWC', 'WIO', 'NWC'), feature_group_count=c) + conv_b
    gate, val = jnp.split(a, 2, axis=-1)
    return (jax.nn.gelu(gate, approximate=False) * val) @ w_down


def setup_inputs(seed: int = 0) -> dict:
    key = jax.random.key(seed)
    ks = jax.random.split(key, 32)
    f32 = jnp.float32
    L, D = DEPTH, D_MODEL
    nrm = lambda k, shape, s: jax.random.normal(k, shape, f32) * s
    return {
        "x": nrm(ks[0], (BATCH, SEQ, D), 1.0),
        "mem": nrm(ks[1], (BATCH, MEM_LEN, D), 1.0),
        "ln_emb_g": 1.0 + nrm(ks[2], (D,), 0.01),
        "ln_emb_b": nrm(ks[3], (D,), 0.01),
        "w_in": nrm(ks[4], (L, D, IN_WIDTH), D ** -0.5),
        "gmlp_ln_g": 1.0 + nrm(ks[5], (L, G_WIDTH), 0.01),
        "gmlp_ln_b": nrm(ks[6], (L, G_WIDTH), 0.01),
        "gmlp_ws": nrm(ks[7], (L, G_HEADS, CHUNK, CHUNK), CHUNK ** -0.5),
        "gmlp_bs": 1.0 + nrm(ks[8], (L, G_HEADS, CHUNK), 0.01),
        "lambda_q1": nrm(ks[9], (L, DA_HEAD_DIM), 0.1),
        "lambda_k1": nrm(ks[10], (L, DA_HEAD_DIM), 0.1),
        "lambda_q2": nrm(ks[11], (L, DA_HEAD_DIM), 0.1),
        "lambda_k2": nrm(ks[12], (L, DA_HEAD_DIM), 0.1),
        "da_subln_g": 1.0 + nrm(ks[13], (L, DA_V_DIM), 0.01),
        "mem_ln_g": 1.0 + nrm(ks[14], (L, D), 0.01),
        "mem_ln_b": nrm(ks[15], (L, D), 0.01),
        "w_mem_kv": nrm(ks[16], (L, D, 2 * M_WIDTH), D ** -0.5),
        "w_out": nrm(ks[17], (L, MIX_WIDTH, D), BETA * MIX_WIDTH ** -0.5),
        "ln1_g": 1.0 + nrm(ks[18], (L, D), 0.01),
        "ln1_b": nrm(ks[19], (L, D), 0.01),
        "w_up": nrm(ks[20], (L, D, 2 * D_FF), D ** -0.5),
        "conv_w": nrm(ks[21], (L, CONV_W, 2 * D_FF), CONV_W ** -0.5),
        "conv_b": nrm(ks[22], (L, 2 * D_FF), 0.01),
        "w_down": nrm(ks[23], (L, D_FF, D), BETA * D_FF ** -0.5),
        "ln2_g": 1.0 + nrm(ks[24], (L, D), 0.01),
        "ln2_b": nrm(ks[25], (L, D), 0.01),
    }


def reference(x, mem, ln_emb_g, ln_emb_b, w_in, gmlp_ln_g, gmlp_ln_b, gmlp_ws, gmlp_bs,
              lambda_q1, lambda_k1, lambda_q2, lambda_k2, da_subln_g, mem_ln_g, mem_ln_b,
              w_mem_kv, w_out, ln1_g, ln1_b, w_up, conv_w, conv_b, w_down, ln2_g, ln2_b):
    B, T, _ = x.shape
    slopes = jnp.asarray(alibi_slopes(DA_HEADS))
    h = layer_norm(x, ln_emb_g, ln_emb_b)
    for l in range(DEPTH):
        lambda_init = 0.8 - 0.6 * math.exp(-0.3 * l)
        proj = h @ w_in[l]
        u, v, q, k, vd, qm = jnp.split(proj, SPLITS, axis=-1)
        y_g = gmlp_group(u, v, gmlp_ws[l], gmlp_bs[l], gmlp_ln_g[l], gmlp_ln_b[l])
        lam = (jnp.exp(jnp.sum(lambda_q1[l] * lambda_k1[l]).astype(jnp.float32))
               - jnp.exp(jnp.sum(lambda_q2[l] * lambda_k2[l]).astype(jnp.float32))
               + lambda_init)
        y_d = diff_attention(q.reshape(B, T, DA_HEADS, 2, DA_HEAD_DIM),
                             k.reshape(B, T, DA_HEADS, 2, DA_HEAD_DIM),
                             vd.reshape(B, T, DA_HEADS, DA_V_DIM), lam, slopes)
        y_d = (rms_norm(y_d, da_subln_g[l]) * (1.0 - lambda_init)).reshape(B, T, DA_WIDTH)
        kv = layer_norm(mem, mem_ln_g[l], mem_ln_b[l]) @ w_mem_kv[l]
        y_m = memory_attention(qm.reshape(B, T, M_HEADS, M_HEAD_DIM), kv)
        y = jnp.concatenate([y_g, y_d, y_m], axis=-1) @ w_out[l]
        h = layer_norm(ALPHA * h + y, ln1_g[l], ln1_b[l])
        f = conv_ffn(h, w_up[l], conv_w[l], conv_b[l], w_down[l])
        h = layer_norm(ALPHA * h + f, ln2_g[l], ln2_b[l])
    return h
```

```python
import numpy as np
import concourse.bass as bass
import concourse.mybir as mybir
from concourse.bass_utils import run_bass_kernel_spmd

F32 = mybir.dt.float32
BF16 = mybir.dt.bfloat16
U8 = mybir.dt.uint8
AF = mybir.ActivationFunctionType
ALU = mybir.AluOpType

D = 1024
T = 4096
NQ = 2048
NOWN = 17
NKT = 32
DFF = 2816
NJ = 22
INW = 2304
ALPHA = 2.0 ** 0.25
EPS = 1e-5
LAM_INIT = 0.2
SBUF_BYTES = 212480
DEBUG = False


class Buf:
    __slots__ = ("name", "last_w", "readers")

    def __init__(self, name):
        self.name = name
        self.last_w = None
        self.readers = []


class Rec:
    ENGS = ("pe", "act", "dve", "pool", "sp")

    def __init__(self, nc):
        self.nc = nc
        self.prog = {e: [] for e in self.ENGS}
        self.count = {e: 0 for e in self.ENGS}
        self.seen = {e: {} for e in self.ENGS}
        self.sems = {}
        self.dma_total = {}
        self.dma_keys = 0

    def sem(self, key):
        if key not in self.sems:
            self.sems[key] = self.nc.alloc_semaphore("s_%s" % (key,))
        return self.sems[key]

    def _deps(self, e, reads, writes, pe_accum=False):
        need = {}

        def add(t):
            if t is None:
                return
            k, v = t
            if need.get(k, 0) < v:
                need[k] = v
        for b in reads:
            add(b.last_w)
        for b in writes:
            if not (pe_accum and b.last_w is not None and b.last_w[0] == "pe" and not b.readers):
                add(b.last_w)
            for t in b.readers:
                add(t)
        waits = []
        for k, v in need.items():
            if k == e and e == "pe":
                continue
            if self.seen[e].get(k, 0) >= v:
                continue
            self.seen[e][k] = v
            waits.append((self.sem(k), v))
        return waits

    def op(self, e, fn, reads=(), writes=(), pe_accum=False):
        waits = self._deps(e, reads, writes, pe_accum)
        self.count[e] += 1
        t = (e, self.count[e])
        sem = self.sem(e)
        for b in reads:
            b.readers.append(t)
        for b in writes:
            b.last_w = t
            b.readers = []

        def emit(h, waits=waits, fn=fn, sem=sem):
            for s, v in waits:
                h.wait_ge(s, v)
            fn(h).then_inc(sem, 1)
        self.prog[e].append(emit)
        return t

    def dma(self, e, out, in_, reads=(), writes=(), key=None):
        waits = self._deps(e, reads, writes)
        if key is None:
            key = "d_" + (writes[0].name if writes else reads[0].name)
        key = key + "_" + e
        sem = self.sem(key)
        self.dma_total[key] = self.dma_total.get(key, 0) + 16
        t = (key, self.dma_total[key])
        for b in reads:
            b.readers.append(t)
        for b in writes:
            b.last_w = t
            b.readers = []

        def emit(h, waits=waits, sem=sem, out=out, in_=in_):
            for s, v in waits:
                h.wait_ge(s, v)
            h.dma_start(out=out, in_=in_).then_inc(sem, 16)
        self.prog[e].append(emit)
        return t

    def barrier(self):
        targets = [(e, self.count[e]) for e in self.ENGS if self.count[e] > 0]
        targets += list(self.dma_total.items())
        for e in self.ENGS:
            waits = []
            for k, v in targets:
                if k == e or self.seen[e].get(k, 0) >= v:
                    continue
                self.seen[e][k] = v
                waits.append((self.sem(k), v))
            if waits:
                def emit(h, waits=waits):
                    for s, v in waits:
                        h.wait_ge(s, v)
                self.prog[e].append(emit)

    def finish(self):
        self.barrier()
        nc = self.nc
        with nc.Block() as block:
            @block.tensor
            def _(h):
                for f in self.prog["pe"]:
                    f(h)

            @block.scalar
            def _(h):
                for f in self.prog["act"]:
                    f(h)

            @block.vector
            def _(h):
                for f in self.prog["dve"]:
                    f(h)

            @block.gpsimd
            def _(h):
                for f in self.prog["pool"]:
                    f(h)

            @block.sync
            def _(h):
                for f in self.prog["sp"]:
                    f(h)


class Ring:
    def __init__(self, items):
        self.items = items
        self.i = 0

    def next(self):
        it = self.items[self.i % len(self.items)]
        self.i += 1
        return it


def build_program():
    nc = bass.Bass("TRN2", target_bir_lowering=False)
    R = Rec(nc)

    def din(name, shape, dt=F32):
        return nc.dram_tensor(name, list(shape), dt, kind="ExternalInput").ap()

    x_d = din("x", [T, D])
    mem_d = din("mem", [256, D])
    w_in_d = din("w_in", [D, INW])
    w_mem_d = din("w_mem_kv", [D, 512])
    w_out_d = din("w_out", [D, D])
    w_up_d = din("w_up", [D, 2 * DFF])
    w_down_d = din("w_down", [DFF, D])
    wsT_d = din("wsT", [128, 4, 128])
    bsT_d = din("bsT", [128, 4])
    cw_d = din("cw", [128, 2 * NJ, 4])
    lnp_d = din("lnp", [128, 6, 8])
    vecs_d = din("vecs", [8, D])
    gml_d = din("gml", [2, 256])
    lam_d = din("lam", [4, 64])
    subg_d = din("subg", [128, 1])
    pos_d = din("pos", [128, T + 2 * (NOWN * 128)], BF16)
    dg_d = din("dg", [128, 4, 128], BF16)
    identf_d = din("identf", [128, 128])
    selc_d = din("selc", [128, 2, 128])
    out_d = nc.dram_tensor("out", [NQ, D], F32, kind="ExternalOutput").ap()
    h1s_d = nc.dram_tensor("h1s", [NOWN * 128, D], F32, kind="Internal").ap()
    h1T_d = nc.dram_tensor("h1Ts", [128, 8, 1 + NOWN * 128], BF16, kind="Internal").ap()
    wup_b = nc.dram_tensor("wup_b", [D, 2 * DFF], BF16, kind="Internal").ap()
    wdn_b = nc.dram_tensor("wdn_b", [DFF, D], BF16, kind="Internal").ap()
    wout_b = nc.dram_tensor("wout_b", [D, D], BF16, kind="Internal").ap()
    win_b = nc.dram_tensor("win_b", [D, INW], BF16, kind="Internal").ap()
    if DEBUG:
        dbg_d = nc.dram_tensor("dbg", [128, 8, 1 + NOWN * 128], BF16, kind="ExternalOutput").ap()

    big = nc.alloc_sbuf_tensor("big", [128, SBUF_BYTES], U8)
    bap = big.ap()
    pst = nc.alloc_psum_tensor("pst", [128, 4096], F32)
    psap = pst.ap()

    def sb(off, free, dt):
        esz = 4 if dt == F32 else 2
        n = int(np.prod(free))
        assert off + n * esz <= SBUF_BYTES, (off, n, esz)
        v = bap[:, off:off + n * esz].bitcast(dt)
        if len(free) == 2:
            v = v.rearrange("p (a b) -> p a b", a=free[0])
        elif len(free) == 3:
            v = v.rearrange("p (a b c) -> p a b c", a=free[0], b=free[1])
        return v

    def bank(i, n=1):
        return psap[:, i * 512:(i + n) * 512]

    PB = [Buf("ps%d" % i) for i in range(8)]

    KT = sb(0, (4, T), BF16)
    KTb = [Buf("KT%d" % i) for i in range(8)]
    Vt = sb(32768, (NKT, 512), BF16)
    Vb = [Buf("V%d" % i) for i in range(NKT)]
    QT = sb(65536, (4, NOWN * 128), BF16)
    QTb = [Buf("QT%d" % i) for i in range(5)]
    QmT = sb(82944, (2, NOWN * 128), BF16)
    QmTb = [Buf("QmT%d" % i) for i in range(5)]
    YW = 1 + NOWN * 128
    yT = sb(91648, (8, YW), BF16)
    yTb = [Buf("yT%d" % i) for i in range(NOWN)]
    P0 = 126976
    CST = SBUF_BYTES - 4096
    identf = sb(CST, (128,), F32)
    lnp = sb(CST + 512, (6, 8), F32)
    epsc = sb(CST + 768, (1,), F32)
    nlam = sb(CST + 772, (1,), F32)
    subg = sb(CST + 776, (1,), F32)
    lamt = sb(CST + 1024, (4, 64), F32)
    lamr = sb(CST + 2048, (4,), F32)
    bsT = sb(CST + 2064, (4,), F32)
    onesb = sb(CST + 2304, (128,), BF16)
    onesf = sb(CST + 2560, (128,), F32)
    zcol = sb(CST + 3072, (8, 1), BF16)
    prs = sb(CST + 3200, (20,), F32)
    pnm = sb(CST + 3280, (20,), F32)
    Bprs = Buf("prs")
    selm = sb(CST + 1024, (2, 128), F32)
    Bc = Buf("consts")

    sp, pool = "sp", "pool"

    R.op("dve", lambda h: h.memset(epsc, EPS), writes=[Bc])
    R.op("dve", lambda h: h.memset(onesb, 1.0), writes=[Bc])
    R.op("dve", lambda h: h.memset(onesf, 1.0), writes=[Bc])
    R.op("dve", lambda h: h.memset(zcol, 0.0), writes=[Bc])
    R.dma("act", identf, identf_d, writes=[Bc])
    R.dma("act", lnp, lnp_d, writes=[Bc])
    R.dma("act", bsT, bsT_d, writes=[Bc])
    Bl = Buf("lamc")

    def late_consts():
        R.dma("act", subg, subg_d, writes=[Bl])
        R.dma("act", lamt.rearrange("p a b -> p (a b)"), lam_d.rearrange("a b -> (a b)").partition_broadcast(128), writes=[Bl])
        R.op("dve", lambda h: h.tensor_tensor(lamt[:, 0, :], lamt[:, 0, :], lamt[:, 1, :], ALU.mult), reads=[Bl], writes=[Bl])
        R.op("dve", lambda h: h.tensor_tensor(lamt[:, 2, :], lamt[:, 2, :], lamt[:, 3, :], ALU.mult), reads=[Bl], writes=[Bl])
        R.op("dve", lambda h: h.reduce_sum(lamr[:, 0:1], lamt[:, 0, :], mybir.AxisListType.X), reads=[Bl], writes=[Bl])
        R.op("dve", lambda h: h.reduce_sum(lamr[:, 1:2], lamt[:, 2, :], mybir.AxisListType.X), reads=[Bl], writes=[Bl])
        R.op("act", lambda h: h.activation(lamr[:, 2:4], lamr[:, 0:2], AF.Exp), reads=[Bl], writes=[Bl])
        R.op("dve", lambda h: h.tensor_tensor(nlam, lamr[:, 3:4], lamr[:, 2:3], ALU.subtract), reads=[Bl], writes=[Bl])
        R.op("dve", lambda h: h.tensor_scalar_add(nlam, nlam, -LAM_INIT), reads=[Bl], writes=[Bl])
        R.op("dve", lambda h: h.tensor_scalar_mul(subg, subg, 1.0 - LAM_INIT), reads=[Bl], writes=[Bl])
        R.dma("act", selm, selc_d, reads=[Bl], writes=[Bl])


    def ln_stats(src, srcb, rstd, nmr, st, mv, tb, n=D):
        nch = (n + 511) // 512
        w = n // nch
        for i in range(nch):
            R.op("dve", lambda h, i=i: h.bn_stats(st[:, i * 6:(i + 1) * 6], src[:, i * w:(i + 1) * w]), reads=[srcb], writes=[tb])
        R.op("dve", lambda h: h.bn_aggr(mv, st[:, 0:6 * nch]), reads=[tb], writes=[tb])
        R.op("act", lambda h: h.activation(rstd, mv[:, 1:2], AF.Sqrt, bias=epsc, scale=1.0), reads=[tb, Bc], writes=[tb])
        R.op("dve", lambda h: h.reciprocal(rstd, rstd), reads=[tb], writes=[tb])
        R.op("dve", lambda h: h.scalar_tensor_tensor(nmr, mv[:, 0:1], -1.0, rstd, ALU.mult, ALU.mult), reads=[tb], writes=[tb])

    def dve_rstd(var_ap, mean_ap, rstd, nmr, t1, t2, tb, iters=6):
        R.op("act", lambda h: h.activation(rstd, var_ap, AF.Sqrt, bias=epsc, scale=1.0), reads=[tb, Bc], writes=[tb])
        R.op("dve", lambda h: h.reciprocal(rstd, rstd), reads=[tb], writes=[tb])
        R.op("dve", lambda h: h.scalar_tensor_tensor(nmr, mean_ap, -1.0, rstd, ALU.mult, ALU.mult), reads=[tb], writes=[tb])

    w_in = sb(P0, (8, INW), BF16)
    Bw_in = Buf("w_in")
    o = P0 + 36864
    xts = [(sb(o + i * 4096, (D,), F32), Buf("xt%d" % i)) for i in range(4)]
    o += 16384
    YT_FREE = 91648 + 2 * YW * 2
    assert YT_FREE + 4 * 4096 <= 126976
    xts += [(sb(YT_FREE + i * 4096, (D,), F32), Buf("xt%d" % (4 + i))) for i in range(4)]
    hTs = [(sb(o + i * 8192, (8, 512), BF16), [Buf("hT%d_%d" % (i, k)) for k in range(4)]) for i in range(2)]
    o += 16384
    gm = []
    for i in range(2):
        gu = sb(o, (256,), F32)
        gv = sb(o + 1024, (256,), F32)
        yg = sb(o + 2048, (256,), F32)
        vln = sb(o + 3072, (256,), BF16)
        gm.append((gu, gv, yg, vln, Buf("gmu%d" % i), Buf("gmv%d" % i), Buf("gmy%d" % i), Buf("gml%d" % i), sb(o, (512,), F32)))
        o += 3584
    gml = sb(o, (2, 256), F32)
    o += 2048
    wsT = sb(o, (4, 128), BF16)
    o += 1024
    sts = []
    for i in range(8):
        sts.append((sb(o, (12,), F32), sb(o + 48, (2,), F32), sb(o + 56, (1,), F32), sb(o + 60, (1,), F32), Buf("st%d" % i)))
        o += 64
    gsts = []
    og = YT_FREE + 4 * 4096
    for i in range(6):
        gsts.append(dict(st=sb(og, (4, 12), F32), mv=sb(og + 192, (4, 2), F32), rstd=sb(og + 224, (4,), F32), nmr=sb(og + 240, (4,), F32),
                         t1=sb(og + 256, (4,), F32), t2=sb(og + 272, (4,), F32), tb=Buf("gst%d" % i)))
        og += 288
    embgb = sb(og, (2, D), F32)
    og += 8192
    assert og <= 126976
    assert o <= CST, o
    Bp1 = Buf("p1c")
    gstring = Ring(gsts)

    stg = [(sb(i * 9216, (INW,), F32), Buf("stg%d" % i)) for i in range(3)]
    Bw_c = [Buf("w_in_c%d" % c) for c in range(8)]

    def load_w_in_chunk(c):
        st_, stb = stg[c % 3]
        R.dma(sp, st_, w_in_d[c * 128:(c + 1) * 128, :], writes=[stb])
        if c % 2 == 0:
            R.op("act", lambda h: h.copy(w_in[:, c, :], st_), reads=[stb], writes=[Bw_c[c]])
        else:
            R.op("dve", lambda h: h.tensor_copy(w_in[:, c, :], st_), reads=[stb], writes=[Bw_c[c]])

    R.dma(pool, wsT, wsT_d, writes=[Bp1])
    R.dma("act", embgb.rearrange("p a b -> p (a b)"), vecs_d[0:2, :].rearrange("a b -> (a b)").partition_broadcast(128), writes=[Bp1])
    R.dma("act", gml.rearrange("p a b -> p (a b)"), gml_d.rearrange("a b -> (a b)").partition_broadcast(128), writes=[Bp1])

    xring = Ring(xts)
    string = Ring(sts)
    gmring = Ring(gm)
    tp_banks = Ring([(0, 2), (2, 2)])
    fm_banks = Ring([4, 5])
    tm_banks = Ring([6, 7])

    def p1_ln_a(t):
        xt, xb = xring.next()
        R.dma(sp, xt, x_d[t * 128:(t + 1) * 128, :], writes=[xb])
        st, mv, rstd, nmr, tb = string.next()
        ln_stats(xt, xb, rstd, nmr, st, mv, tb)
        R.op("act", lambda h: h.activation(xt, xt, AF.Identity, bias=nmr, scale=rstd), reads=[xb, tb], writes=[xb])
        return xt, xb

    def p1_ln_t(xt, xb, hT, hTb_k, k):
        b0, _ = tp_banks.next()
        ps = bank(b0, 2)
        for c in range(8):
            R.op("pe", lambda h, c=c: h.transpose(ps[:, c * 128:(c + 1) * 128], xt[:, c * 128:(c + 1) * 128], identf),
                 reads=[xb, Bc], writes=[PB[b0 + c // 4]])
        for hb_ in range(2):
            src = ps[:, hb_ * 512:(hb_ + 1) * 512].rearrange("p (a b) -> p a b", a=4)
            dst = hT[:, hb_ * 4:(hb_ + 1) * 4, k * 128:(k + 1) * 128]
            if hb_ == 0:
                R.op("act", lambda h, src=src, dst=dst: h.copy(dst, src), reads=[PB[b0 + hb_]], writes=[hTb_k])
            else:
                R.op("dve", lambda h, src=src, dst=dst: h.tensor_copy(dst, src), reads=[PB[b0 + hb_]], writes=[hTb_k])

    evtog = [0]

    def fm_proj(hT, hTbs, ntok, col0, dst, dstb, scale=None):
        b = fm_banks.next()
        ps = bank(b)[:, 0:ntok]
        nk = (ntok + 127) // 128
        for c in range(8):
            R.op("pe", lambda h, c=c: h.matmul(ps, w_in[:, c, col0:col0 + 128], hT[:, c, 0:ntok], start=(c == 0), stop=(c == 7)),
                 reads=[Bw_c[c]] + hTbs[:nk], writes=[PB[b]], pe_accum=(c > 0))
        evtog[0] ^= 1
        if scale is None:
            if evtog[0]:
                R.op("act", lambda h: h.copy(dst, ps), reads=[PB[b]], writes=[dstb])
            else:
                R.op("dve", lambda h: h.tensor_copy(dst, ps), reads=[PB[b]], writes=[dstb])
        else:
            if evtog[0]:
                R.op("act", lambda h: h.mul(dst, ps, scale), reads=[PB[b]], writes=[dstb])
            else:
                R.op("dve", lambda h: h.tensor_scalar_mul(dst, ps, scale), reads=[PB[b]], writes=[dstb])

    def tm_proj(hT, hTb_k, k, col0):
        b = tm_banks.next()
        ps = bank(b)
        for c in range(8):
            R.op("pe", lambda h, c=c: h.matmul(ps, hT[:, c, k * 128:(k + 1) * 128], w_in[:, c, col0:col0 + 512], start=(c == 0), stop=(c == 7)),
                 reads=[Bw_c[c], hTb_k], writes=[PB[b]], pe_accum=(c > 0))
        return b, ps

    def v_proj(t, hT, hTb_k, k):
        b, ps = tm_proj(hT, hTb_k, k, 1536)
        evtog[0] ^= 1
        if evtog[0]:
            R.op("dve", lambda h: h.tensor_copy(Vt[:, t, :], ps), reads=[PB[b]], writes=[Vb[t]])
        else:
            R.op("act", lambda h: h.copy(Vt[:, t, :], ps), reads=[PB[b]], writes=[Vb[t]])

    def gm1(t, hT, hTb_k, k):
        b, ps = tm_proj(hT, hTb_k, k, 0)
        gu, gv, yg, vln, gub, gvb, gyb, glb, guv = gmring.next()
        R.op("act", lambda h: h.activation(guv, ps, AF.Gelu), reads=[PB[b]], writes=[gub, gvb])
        G = gmG[0]
        kk = gmG[1]
        gmG[1] += 1
        st, mv, tb = G["st"], G["mv"], G["tb"]
        R.op("dve", lambda h: h.bn_stats(st[:, kk, 0:6], gv), reads=[gvb], writes=[tb])
        R.op("dve", lambda h: h.bn_aggr(mv[:, kk, :], st[:, kk, 0:6]), reads=[tb], writes=[tb])
        R.op("dve", lambda h: h.scalar_tensor_tensor(gv, gv, mv[:, kk, 0:1], gml[:, 0, :], ALU.subtract, ALU.mult), reads=[gvb, tb, Bp1], writes=[gvb])
        return (t, gu, gv, yg, vln, gub, gvb, gyb, glb, G, kk, None, tb)

    gmG = [None, 0]

    def gm_begin():
        gmG[0] = gstring.next()
        gmG[1] = 0

    def gm_rstd(n):
        G = gmG[0]
        dve_rstd(G["mv"][:, 0:n, 1], G["mv"][:, 0:n, 0], G["rstd"][:, 0:n], G["nmr"][:, 0:n], G["t1"][:, 0:n], G["t2"][:, 0:n], G["tb"])

    def gm2(ctx):
        (t, gu, gv, yg, vln, gub, gvb, gyb, glb, G, kk, _, tb) = ctx
        R.op("dve", lambda h: h.scalar_tensor_tensor(vln, gv, G["rstd"][:, kk:kk + 1], gml[:, 1, :], ALU.mult, ALU.add), reads=[gvb, tb, Bp1], writes=[glb])

    def gm3(ctx):
        (t, gu, gv, yg, vln, gub, gvb, gyb, glb, G, kk, _, tb) = ctx
        sbk = tm_banks.next()
        pss = bank(sbk)
        for g in range(4):
            R.op("pe", lambda h, g=g: h.matmul(pss[:, g * 64:(g + 1) * 64], wsT[:, g, :], vln[:, g * 64:(g + 1) * 64], start=True, stop=True),
                 reads=[glb, Bp1], writes=[PB[sbk]])
        for g in range(4):
            R.op("dve", lambda h, g=g: h.scalar_tensor_tensor(yg[:, g * 64:(g + 1) * 64], pss[:, g * 64:(g + 1) * 64], bsT[:, g:g + 1],
                                                              gu[:, g * 64:(g + 1) * 64], ALU.add, ALU.mult),
                 reads=[PB[sbk], gub, Bc], writes=[gyb])
        for c in range(2):
            R.op("pe", lambda h, c=c: h.transpose(pss[:, 256 + c * 128:256 + (c + 1) * 128], yg[:, c * 128:(c + 1) * 128], identf),
                 reads=[gyb, Bc], writes=[PB[sbk]])
        R.op("dve", lambda h: h.tensor_copy(yT[:, 0:2, 1 + t * 128:1 + (t + 1) * 128], pss[:, 256:512].rearrange("p (a b) -> p a b", a=2)),
             reads=[PB[sbk]], writes=[yTb[t]])

    hring = Ring(hTs)
    groups = [list(range(g * 4, g * 4 + 4)) for g in range(8)]
    lnA = {}
    lnG = {}

    def do_ln_a(g):
        G = gstring.next()
        st, mv, tb = G["st"], G["mv"], G["tb"]
        res = []
        for k, t in enumerate(groups[g]):
            xt, xb = xring.next()
            R.dma(sp, xt, x_d[t * 128:(t + 1) * 128, :], writes=[xb])
            for i in range(2):
                R.op("dve", lambda h, i=i, k=k, xt=xt: h.bn_stats(st[:, k, i * 6:(i + 1) * 6], xt[:, i * 512:(i + 1) * 512]), reads=[xb], writes=[tb])
            R.op("dve", lambda h, k=k: h.bn_aggr(mv[:, k, :], st[:, k, :]), reads=[tb], writes=[tb])
            res.append((xt, xb))
        lnA[g] = res
        lnG[g] = G

    def do_ln_b(g):
        G = lnG[g]
        mv, tb = G["mv"], G["tb"]
        res = lnA[g]
        dve_rstd(mv[:, :, 1], mv[:, :, 0], G["rstd"], G["nmr"], G["t1"], G["t2"], tb)
        for k in range(4):
            xt, xb = res[k]
            R.op("dve", lambda h, k=k, xt=xt: h.tensor_scalar(xt, xt, G["rstd"][:, k:k + 1], G["nmr"][:, k:k + 1], ALU.mult, ALU.add),
                 reads=[xb, tb], writes=[xb])
            R.op("pool", lambda h, xt=xt: h.tensor_tensor(xt, xt, embgb[:, 0, :], ALU.mult), reads=[xb, Bp1], writes=[xb])
            R.op("pool", lambda h, xt=xt: h.tensor_tensor(xt, xt, embgb[:, 1, :], ALU.add), reads=[xb, Bp1], writes=[xb])
        if g <= 4:
            R.op("dve", lambda h: h.tensor_copy(prs[:, g * 4:(g + 1) * 4], G["rstd"]), reads=[tb], writes=[Bprs])
            R.op("dve", lambda h: h.tensor_copy(pnm[:, g * 4:(g + 1) * 4], G["nmr"]), reads=[tb], writes=[Bprs])

    def do_ln_t(g):
        hT, hTbs = hring.next()
        for k in range(4):
            xt, xb = lnA[g][k]
            p1_ln_t(xt, xb, hT, hTbs[k], k)
        return hT, hTbs

    def proj_pieces(g, hT, hTbs):
        tiles = groups[g]
        t0 = tiles[0] * 128
        own = g <= 4
        ntok = 512 if g < 4 else 128
        nfull = 4 if g < 4 else (1 if g == 4 else 0)
        P = []
        st = {}

        def mk_gm1(k, key):
            def f():
                if key not in st:
                    gm_begin()
                    st[key] = []
                st[key].append(gm1(tiles[k], hT, hTbs[k], k))
            return f

        def mk_gm2(key):
            def f():
                if st.get(key):
                    gm_rstd(len(st[key]))
                    for c_ in st[key]:
                        gm2(c_)
            return f

        def mk_gm3(key, i):
            def f():
                if st.get(key) and i < len(st[key]):
                    gm3(st[key][i])
            return f

        if own:
            for k in range(min(2, nfull)):
                P.append(mk_gm1(k, "a"))
        for hh in range(4):
            P.append(lambda hh=hh: fm_proj(hT, hTbs, 512, 1024 + hh * 128, KT[:, hh, t0:t0 + 512], KTb[g]))
        if own:
            P.append(mk_gm2("a"))
        for k, t in enumerate(tiles):
            P.append(lambda k=k, t=t: v_proj(t, hT, hTbs[k], k))
        if own:
            P.append(mk_gm3("a", 0))
            P.append(mk_gm3("a", 1))
            for k in range(2, nfull):
                P.append(mk_gm1(k, "b"))
            for hh in range(4):
                P.append(lambda hh=hh: fm_proj(hT, hTbs, ntok, 512 + hh * 128, QT[:, hh, t0:t0 + ntok], QTb[g], scale=2.0 ** (-3 + 2 * (hh + 1))))
            P.append(mk_gm2("b"))
            for cc in range(2):
                P.append(lambda cc=cc: fm_proj(hT, hTbs, ntok, 2048 + cc * 128, QmT[:, cc, t0:t0 + ntok], QmTb[g], scale=0.125))
            P.append(mk_gm3("b", 0))
            P.append(mk_gm3("b", 1))
        return P

    do_ln_a(0)
    do_ln_b(0)
    for c in range(8):
        load_w_in_chunk(c)
    do_ln_a(1)
    do_ln_b(1)
    cur = do_ln_t(0)
    for g in range(8):
        P = proj_pieces(g, cur[0], cur[1])
        n = len(P)
        nxt = None
        if g + 1 < 8:
            hTn, hTbn = hring.next()
            nxt = (hTn, hTbn)
            marks = {max(1, round((k + 1) * n / 6.0)): k for k in range(4)}
        else:
            marks = {}
        if g + 2 < 8:
            do_ln_a(g + 2)
        for i, f in enumerate(P):
            if i in marks:
                k = marks[i]
                xt, xb = lnA[g + 1][k]
                p1_ln_t(xt, xb, hTn, hTbn[k], k)
            f()
            if g + 2 < 8 and i == n // 2:
                do_ln_b(g + 2)
        cur = nxt

    late_consts()
    R.barrier()

    o = P0
    NPQ = NOWN * 128
    pos = sb(o, (T + 2 * NPQ,), BF16)
    o += (T + 2 * NPQ) * 2
    PK = pos[:, 0:T]
    PQL = pos[:, T:T + NPQ]
    PQR = pos[:, T + NPQ:T + 2 * NPQ]
    dg = sb(o, (4, 128), BF16)
    o += 1024
    Bpos = Buf("pos")
    ETall = sb(o, (4, 1024), BF16)
    ETs = [(sb(o + i * 2048, (1024,), BF16), Buf("ET%d" % i)) for i in range(4)]
    o += 8192
    yacc = [(sb(o + i * 2048, (512,), F32), Buf("yacc%d" % i)) for i in range(4)]
    o += 8192
    sqs = [(sb(o + i * 2048, (512,), F32), Buf("sq%d" % i)) for i in range(4)]
    o += 8192
    tmps = [(sb(o + i * 2048, (512,), F32), Buf("tmp%d" % i)) for i in range(4)]
    o += 8192
    w_mem = sb(o, (8, 512), BF16)
    o += 8192
    memT = sb(o, (8, 256), BF16)
    o += 4096
    KmT = sb(o, (2, 256), BF16)
    o += 1024
    VmP = sb(o, (2, 4, 128), BF16)
    o += 2048
    onesP = sb(o, (2, 128), BF16)
    o += 512
    mts = [(sb(o + i * 4096, (D,), F32), Buf("mt%d" % i)) for i in range(2)]
    o += 8192
    assert o <= CST, o
    Bmem = Buf("mem")
    BmT = Buf("memT")
    Bkm = Buf("KmT")
    Bvm = Buf("VmP")
    Bon = Buf("onesP")
    Bwm = Buf("w_mem")

    R.dma(sp, pos, pos_d, writes=[Bpos])
    R.dma(sp, dg, dg_d, writes=[Bpos])
    for c in range(8):
        R.dma(pool, w_mem[:, c, :], w_mem_d[c * 128:(c + 1) * 128, :], writes=[Bwm])
    Bs_wout = Buf("s_wout")
    Bs_wup = [Buf("s_wup%d" % c) for c in range(8)]
    Bs_wdn = Buf("s_wdn")
    for c in range(8):
        R.dma(pool, wout_b[c * 128:(c + 1) * 128, :], w_out_d[c * 128:(c + 1) * 128, :], writes=[Bs_wout])
    for j in range(NJ):
        R.dma(pool, wdn_b[j * 128:(j + 1) * 128, :], w_down_d[j * 128:(j + 1) * 128, :], writes=[Bs_wdn])
    for c in range(8):
        R.dma(pool, wup_b[c * 128:(c + 1) * 128, :], w_up_d[c * 128:(c + 1) * 128, :], writes=[Bs_wup[c]])
    R.op("dve", lambda h: h.memset(VmP, 0.0), writes=[Bvm])
    R.op("dve", lambda h: h.memset(onesP, 0.0), writes=[Bon])
    R.op("dve", lambda h: h.memset(onesP[:, 0, 0:64], 1.0), writes=[Bon])
    R.op("dve", lambda h: h.memset(onesP[:, 1, 64:128], 1.0), writes=[Bon])

    for mt in range(2):
        xt, xb = mts[mt]
        R.dma(sp, xt, mem_d[mt * 128:(mt + 1) * 128, :], writes=[xb])
        st, mv, rstd, nmr, tb = sts[mt]
        ln_stats(xt, xb, rstd, nmr, st, mv, tb)
        R.op("act", lambda h, xt=xt, nmr=nmr, rstd=rstd: h.activation(xt, xt, AF.Identity, bias=nmr, scale=rstd), reads=[xb, tb], writes=[xb])
        ps = bank(2 * mt, 2)
        for c in range(8):
            R.op("pe", lambda h, c=c, ps=ps, xt=xt: h.transpose(ps[:, c * 128:(c + 1) * 128], xt[:, c * 128:(c + 1) * 128], identf),
                 reads=[xb, Bc], writes=[PB[2 * mt + c // 4]])
        for c in range(8):
            R.op("dve", lambda h, c=c, ps=ps, mt=mt: h.tensor_scalar(memT[:, c, mt * 128:(mt + 1) * 128], ps[:, c * 128:(c + 1) * 128],
                                                                      lnp[:, 4, c:c + 1], lnp[:, 5, c:c + 1], ALU.mult, ALU.add),
                 reads=[PB[2 * mt + c // 4], Bc], writes=[BmT])
    for cc in range(2):
        ps = bank(4 + cc)[:, 0:256]
        for c in range(8):
            R.op("pe", lambda h, c=c, ps=ps, cc=cc: h.matmul(ps, w_mem[:, c, cc * 128:(cc + 1) * 128], memT[:, c, :], start=(c == 0), stop=(c == 7)),
                 reads=[Bwm, BmT], writes=[PB[4 + cc]], pe_accum=(c > 0))
        R.op("dve", lambda h, ps=ps, cc=cc: h.tensor_copy(KmT[:, cc, :], ps), reads=[PB[4 + cc]], writes=[Bkm])
    for mt in range(2):
        ps = bank(6 + mt)[:, 0:256]
        for c in range(8):
            R.op("pe", lambda h, c=c, ps=ps, mt=mt: h.matmul(ps, memT[:, c, mt * 128:(mt + 1) * 128], w_mem[:, c, 256:512], start=(c == 0), stop=(c == 7)),
                 reads=[Bwm, BmT], writes=[PB[6 + mt]], pe_accum=(c > 0))
        for hh in range(4):
            R.op("dve", lambda h, ps=ps, mt=mt, hh=hh: h.tensor_copy(VmP[:, mt, hh, (hh % 2) * 64:(hh % 2) * 64 + 64], ps[:, hh * 64:(hh + 1) * 64]),
                 reads=[PB[6 + mt]], writes=[Bvm])

    etring = Ring(ETs)
    tmpring = Ring(tmps)

    chunks = [(0, 4), (4, 4), (8, 4), (12, 4), (16, 1)]

    def mem_qk(ci, hp):
        qt0, nt = chunks[ci]
        nq = nt * 128
        q0 = qt0 * 128
        for hl in range(2):
            for mt in range(2):
                bb = hl * 2 + mt
                R.op("pe", lambda h, bb=bb, hl=hl, mt=mt: h.matmul(bank(bb)[:, 0:nq], KmT[hl * 64:(hl + 1) * 64, hp, mt * 128:(mt + 1) * 128],
                                                                   QmT[hl * 64:(hl + 1) * 64, hp, q0:q0 + nq], start=True, stop=True),
                     reads=[Bkm, QmTb[ci]], writes=[PB[bb]])

    def mem_exp(ci, hp):
        nq = chunks[ci][1] * 128
        R.op("act", lambda h: h.activation(ETall[:, :, 0:nq], bank(0, 4).rearrange("p (a b) -> p a b", a=4)[:, :, 0:nq], AF.Exp),
             reads=[PB[0], PB[1], PB[2], PB[3]], writes=[ETs[i][1] for i in range(4)])

    def mem_av(ci, hp, ob):
        nq = chunks[ci][1] * 128
        pso = bank(ob)[:, 0:nq]
        pss = bank(ob + 1)[:, 0:nq]
        for i in range(4):
            hl, mt = i // 2, i % 2
            et, eb = ETs[i]
            R.op("pe", lambda h, et=et, hl=hl, mt=mt, i=i: h.matmul(pso, VmP[:, mt, hp * 2 + hl, :], et[:, 0:nq], start=(i == 0), stop=(i == 3)),
                 reads=[Bvm, eb], writes=[PB[ob]], pe_accum=(i > 0))
        for i in range(4):
            hl = i // 2
            et, eb = ETs[i]
            R.op("pe", lambda h, et=et, hl=hl, i=i: h.matmul(pss, onesP[:, hl, :], et[:, 0:nq], start=(i == 0), stop=(i == 3)),
                 reads=[Bon, eb], writes=[PB[ob + 1]], pe_accum=(i > 0))

    def mem_tail(ci, hp, ob):
        qt0, nt = chunks[ci]
        nq = nt * 128
        q0 = qt0 * 128
        pso = bank(ob)[:, 0:nq]
        pss = bank(ob + 1)[:, 0:nq]
        tm, tmb = tmpring.next()
        R.op("dve", lambda h: h.reciprocal(tm[:, 0:nq], pss), reads=[PB[ob + 1]], writes=[tmb])
        for k in range(nt):
            R.op("dve", lambda h, k=k: h.tensor_tensor(yT[:, 6 + hp, 1 + q0 + k * 128:1 + q0 + (k + 1) * 128],
                                                       pso[:, k * 128:(k + 1) * 128], tm[:, k * 128:(k + 1) * 128], ALU.mult),
                 reads=[PB[ob], tmb], writes=[yTb[qt0 + k]])

    mblocks = [(ci, hp) for ci in range(len(chunks)) for hp in range(2)]
    mem_qk(*mblocks[0])
    mem_exp(*mblocks[0])
    for bi, (ci, hp) in enumerate(mblocks):
        ob = 4 + 2 * (bi % 2)
        mem_av(ci, hp, ob)
        if bi + 1 < len(mblocks):
            mem_qk(*mblocks[bi + 1])
            mem_exp(*mblocks[bi + 1])
        mem_tail(ci, hp, ob)

    SB = [[0, 1], [2, 3]]
    OB = [4, 5]
    UB = [6, 7]
    slopes = [2.0 ** (-2 * (i + 1)) for i in range(4)]

    def qk(ci, hh, kt, buf):
        qt0, nt = chunks[ci]
        nq = nt * 128
        q0 = qt0 * 128
        for m in range(2):
            b = SB[buf][m]
            ps = bank(b)[:, 0:nq]
            r0 = 64 * m
            R.op("pe", lambda h, ps=ps, r0=r0: h.matmul(ps, KT[r0:r0 + 64, hh, kt * 128:(kt + 1) * 128], QT[r0:r0 + 64, hh, q0:q0 + nq],
                                                        start=True, stop=False),
                 reads=[KTb[kt // 4], QTb[ci]], writes=[PB[b]])
            pk = PK[r0:r0 + 64, kt * 128:(kt + 1) * 128]
            if kt < qt0:
                R.op("pe", lambda h, ps=ps, pk=pk, r0=r0: h.matmul(ps, pk, PQL[r0:r0 + 64, q0:q0 + nq], start=False, stop=True),
                     reads=[Bpos], writes=[PB[b]], pe_accum=True)
            elif kt >= qt0 + nt:
                R.op("pe", lambda h, ps=ps, pk=pk, r0=r0: h.matmul(ps, pk, PQR[r0:r0 + 64, q0:q0 + nq], start=False, stop=True),
                     reads=[Bpos], writes=[PB[b]], pe_accum=True)
            else:
                i = kt - qt0
                if i > 0:
                    R.op("pe", lambda h, ps=ps, pk=pk, r0=r0, i=i: h.matmul(ps[:, 0:i * 128], pk, PQR[r0:r0 + 64, q0:q0 + i * 128], start=False, stop=False),
                         reads=[Bpos], writes=[PB[b]], pe_accum=True)
                if i < nt - 1:
                    R.op("pe", lambda h, ps=ps, pk=pk, r0=r0, i=i: h.matmul(ps[:, (i + 1) * 128:nq], pk, PQL[r0:r0 + 64, q0 + (i + 1) * 128:q0 + nq], start=False, stop=False),
                         reads=[Bpos], writes=[PB[b]], pe_accum=True)
                for a in range(2):
                    R.op("pe", lambda h, ps=ps, r0=r0, i=i, a=a: h.matmul(ps[:, i * 128:(i + 1) * 128], dg[r0:r0 + 64, a, :], dg[r0:r0 + 64, 2 + a, :],
                                                                          start=False, stop=(a == 1)),
                         reads=[Bpos], writes=[PB[b]], pe_accum=True)

    def expo(ci, hh, kt, buf):
        nq = chunks[ci][1] * 128
        et, eb = etring.next()
        b0 = SB[buf][0]
        if nq == 512:
            R.op("act", lambda h: h.activation(et[:, 0:1024], bank(b0, 2), AF.Exp, scale=slopes[hh]), reads=[PB[b0], PB[b0 + 1]], writes=[eb])
        else:
            R.op("act", lambda h: h.activation(et.rearrange("p (a b) -> p a b", a=2)[:, :, 0:nq], bank(b0, 2).rearrange("p (a b) -> p a b", a=2)[:, :, 0:nq],
                                               AF.Exp, scale=slopes[hh]),
                 reads=[PB[b0], PB[b0 + 1]], writes=[eb])
        return [(et[:, 0:512], eb), (et[:, 512:1024], eb)]

    def av_a(ci, hh, kt, ets):
        nq = chunks[ci][1] * 128
        for m in range(2):
            et, eb = ets[m]
            R.op("pe", lambda h, et=et, m=m: h.matmul(bank(OB[m])[:, 0:nq], Vt[:, kt, hh * 128:(hh + 1) * 128], et[:, 0:nq], start=(kt == 0), stop=(kt == NKT - 1)),
                 reads=[Vb[kt], eb], writes=[PB[OB[m]]], pe_accum=(kt > 0))

    def av_b(ci, hh, kp, e0, e1):
        nq = chunks[ci][1] * 128
        for j, (et, eb) in enumerate((e0[0], e0[1], e1[0], e1[1])):
            R.op("pe", lambda h, et=et, j=j: h.matmul(bank(UB[0])[32 * j:32 * j + 32, 0:nq], onesb[:, 0:32], et[:, 0:nq],
                                                      start=(kp == 0), stop=(kp == NKT // 2 - 1), tile_position=(0, 32 * j)),
                 reads=[Bc, eb], writes=[PB[UB[0]]], pe_accum=(kp > 0 or j > 0))

    def post_a(ci, hh):
        nq = chunks[ci][1] * 128
        o0, o0b = tmpring.next()
        o1, o1b = tmpring.next()
        r0, r0b = tmpring.next()
        r1, r1b = tmpring.next()
        y, yb = yacc[hh]
        sq, sqb = sqs[hh]
        R.op("act", lambda h: h.copy(o0[:, 0:nq], bank(OB[0])[:, 0:nq]), reads=[PB[OB[0]]], writes=[o0b])
        R.op("act", lambda h: h.copy(o1[:, 0:nq], bank(OB[1])[:, 0:nq]), reads=[PB[OB[1]]], writes=[o1b])
        R.op("dve", lambda h: h.tensor_copy(sq[:, 0:nq], bank(UB[0])[:, 0:nq]), reads=[PB[UB[0]]], writes=[sqb])
        R.op("pe", lambda h: h.matmul(bank(UB[0])[:, 0:nq], selm[:, 1, :], sq[:, 0:nq], start=True, stop=True), reads=[Bc, sqb], writes=[PB[UB[0]]])
        R.op("pe", lambda h: h.matmul(bank(UB[1])[:, 0:nq], selm[:, 0, :], sq[:, 0:nq], start=True, stop=True), reads=[Bc, sqb], writes=[PB[UB[1]]])
        R.op("dve", lambda h: h.reciprocal(r1[:, 0:nq], bank(UB[0])[:, 0:nq]), reads=[PB[UB[0]]], writes=[r1b])
        R.op("dve", lambda h: h.reciprocal(r0[:, 0:nq], bank(UB[1])[:, 0:nq]), reads=[PB[UB[1]]], writes=[r0b])
        R.op("dve", lambda h: h.tensor_tensor(y[:, 0:nq], o0[:, 0:nq], r0[:, 0:nq], ALU.mult), reads=[o0b, r0b], writes=[yb])
        R.op("dve", lambda h: h.tensor_tensor(r1[:, 0:nq], o1[:, 0:nq], r1[:, 0:nq], ALU.mult), reads=[o1b, r1b], writes=[r1b])
        R.op("dve", lambda h: h.scalar_tensor_tensor(y[:, 0:nq], r1[:, 0:nq], nlam, y[:, 0:nq], ALU.mult, ALU.add), reads=[r1b, yb, Bc], writes=[yb])
        R.op("pool", lambda h: h.tensor_tensor(sq[:, 0:nq], y[:, 0:nq], y[:, 0:nq], ALU.mult), reads=[yb], writes=[sqb])

    def post_b(ci):
        qt0, nt = chunks[ci]
        nq = nt * 128
        q0 = qt0 * 128
        for hh in range(4):
            sq, sqb = sqs[hh]
            R.op("pe", lambda h, hh=hh, sq=sq: h.matmul(bank(hh)[:, 0:nq], onesf, sq[:, 0:nq], start=True, stop=True), reads=[Bc, sqb], writes=[PB[hh]])
        for hh in range(4):
            sq, sqb = sqs[hh]
            R.op("act", lambda h, hh=hh, sq=sq: h.activation(sq[:, 0:nq], bank(hh)[:, 0:nq], AF.Sqrt, bias=epsc, scale=1.0 / 128.0), reads=[PB[hh], Bc], writes=[sqb])
        for hh in range(4):
            sq, sqb = sqs[hh]
            y, yb = yacc[hh]
            R.op("dve", lambda h, sq=sq: h.reciprocal(sq[:, 0:nq], sq[:, 0:nq]), reads=[sqb], writes=[sqb])
            for k in range(nt):
                R.op("dve", lambda h, k=k, hh=hh, sq=sq, y=y: h.scalar_tensor_tensor(yT[:, 2 + hh, 1 + q0 + k * 128:1 + q0 + (k + 1) * 128], y[:, k * 128:(k + 1) * 128], subg,
                                                                                  sq[:, k * 128:(k + 1) * 128], ALU.mult, ALU.mult),
                     reads=[yb, sqb, Bc], writes=[yTb[qt0 + k]])

    for ci in range(len(chunks)):
        for hh in range(4):
            qk(ci, hh, 0, 0)
            qk(ci, hh, 1, 1)
            for kp in range(NKT // 2):
                k0, k1 = 2 * kp, 2 * kp + 1
                e0 = expo(ci, hh, k0, 0)
                e1 = expo(ci, hh, k1, 1)
                if k1 + 1 < NKT:
                    qk(ci, hh, k0 + 2, 0)
                    qk(ci, hh, k1 + 2, 1)
                av_a(ci, hh, k0, e0)
                av_a(ci, hh, k1, e1)
                av_b(ci, hh, kp, e0, e1)
            post_a(ci, hh)
        post_b(ci)

    R.barrier()
    if DEBUG:
        R.dma(sp, dbg_d, yT, reads=yTb, writes=[Buf("dbg")])
        R.barrier()

    w_down = sb(0, (NJ, D), BF16)
    w_up = sb(45056, (8, 2 * DFF), BF16)
    Bwd = Buf("w_down")
    Bwu = [Buf("w_up%d" % c) for c in range(8)]
    o = P0
    w_out = sb(o, (8, D), BF16)
    o += 16384
    Bwo = Buf("w_out")
    vecs = sb(o, (4, D), F32)
    o += 16384
    Bvec = Buf("vecs")
    xa = [(sb(o + i * 4096, (D,), F32), Buf("xa%d" % i)) for i in range(4)]
    o += 16384
    ra = [(sb(o + i * 4096, (D,), F32), Buf("ra%d" % i)) for i in range(3)]
    o += 12288
    h1Tt = [(sb(o + i * 2048, (8, 128), BF16), Buf("h1Tt%d" % i)) for i in range(2)]
    o += 4096
    sts3 = []
    for i in range(8):
        sts3.append((sb(o, (12,), F32), sb(o + 48, (2,), F32), sb(o + 56, (1,), F32), sb(o + 60, (1,), F32), Buf("st3_%d" % i)))
        o += 64
    assert o <= CST, o

    for c in range(8):
        R.dma(sp, w_out[:, c, :], wout_b[c * 128:(c + 1) * 128, :], reads=[Bs_wout], writes=[Bwo])
    R.dma(sp, vecs.rearrange("p a b -> p (a b)"), vecs_d[0:4, :].rearrange("a b -> (a b)").partition_broadcast(128), writes=[Bvec])
    wq = []
    for j in range(NJ):
        wq.append(lambda j=j: R.dma(sp, w_down[:, j, :], wdn_b[j * 128:(j + 1) * 128, :], reads=[Bs_wdn], writes=[Bwd]))
    for c in range(4):
        for hf in range(2):
            wq.append(lambda c=c, hf=hf: R.dma(sp, w_up[:, c, hf * DFF:(hf + 1) * DFF], wup_b[c * 128:(c + 1) * 128, hf * DFF:(hf + 1) * DFF], reads=[Bs_wup[c]], writes=[Bwu[c]]))
    Bh1s = [Buf("h1s%d" % t) for t in range(NOWN)]
    Bh1T = [Buf("h1Ts%d" % t) for t in range(NOWN)]

    xaring = Ring(xa)
    raring = Ring(ra)
    st3ring = Ring(sts3)
    h1ring = Ring(h1Tt)
    zb = Ring([(0, 2), (2, 2)])
    tb3 = Ring([(4, 2), (6, 2)])

    def p3a_A1(t):
        xt, xb = xaring.next()
        R.dma(sp, xt, x_d[t * 128:(t + 1) * 128, :], writes=[xb])
        return (xt, xb, t)

    def rstd_chain(mv, rstd, nmr, tb):
        R.op("act", lambda h: h.activation(rstd, mv[:, 1:2], AF.Sqrt, bias=epsc, scale=1.0), reads=[tb, Bc], writes=[tb])
        R.op("dve", lambda h: h.reciprocal(rstd, rstd), reads=[tb], writes=[tb])
        R.op("dve", lambda h: h.scalar_tensor_tensor(nmr, mv[:, 0:1], -1.0, rstd, ALU.mult, ALU.mult), reads=[tb], writes=[tb])

    def p3a_A2(ctx):
        xt, xb, t = ctx
        R.op("act", lambda h: h.activation(xt, xt, AF.Identity, bias=pnm[:, t:t + 1], scale=prs[:, t:t + 1]), reads=[xb, Bprs], writes=[xb])
        R.op("pool", lambda h: h.tensor_tensor(xt, xt, vecs[:, 0, :], ALU.mult), reads=[xb, Bvec], writes=[xb])
        R.op("pool", lambda h: h.tensor_tensor(xt, xt, vecs[:, 1, :], ALU.add), reads=[xb, Bvec], writes=[xb])

    def p3a_Z(t):
        b0, _ = zb.next()
        psz = bank(b0, 2)
        for hf in range(2):
            for c in range(8):
                R.op("pe", lambda h, c=c, hf=hf: h.matmul(psz[:, hf * 512:(hf + 1) * 512], yT[:, c, 1 + t * 128:1 + (t + 1) * 128],
                                                          w_out[:, c, hf * 512:(hf + 1) * 512], start=(c == 0), stop=(c == 7)),
                     reads=[yTb[t], Bwo], writes=[PB[b0 + hf]], pe_accum=(c > 0))
        return b0, psz

    def p3a_C1(t, xt, xb, b0, psz):
        rt, rb = raring.next()
        R.op("dve", lambda h: h.scalar_tensor_tensor(rt, xt, ALPHA, psz, ALU.mult, ALU.add), reads=[xb, PB[b0], PB[b0 + 1]], writes=[rb])
        st, mv, rstd, nmr, tb = st3ring.next()
        for i in range(2):
            R.op("dve", lambda h, i=i: h.bn_stats(st[:, i * 6:(i + 1) * 6], rt[:, i * 512:(i + 1) * 512]), reads=[rb], writes=[tb])
        R.op("dve", lambda h: h.bn_aggr(mv, st[:, 0:12]), reads=[tb], writes=[tb])
        return (rt, rb, mv, rstd, nmr, tb)

    def p3a_C2a(ctx):
        rt, rb, mv, rstd, nmr, tb = ctx
        rstd_chain(mv, rstd, nmr, tb)
        R.op("act", lambda h: h.activation(rt, rt, AF.Identity, bias=nmr, scale=rstd), reads=[rb, tb], writes=[rb])

    def p3a_C2b(t, ctx):
        rt, rb, mv, rstd, nmr, tb = ctx
        c0, _ = tb3.next()
        pst_ = bank(c0, 2)
        for c in range(8):
            R.op("pe", lambda h, c=c: h.transpose(pst_[:, c * 128:(c + 1) * 128], rt[:, c * 128:(c + 1) * 128], identf),
                 reads=[rb, Bc], writes=[PB[c0 + c // 4]])
        ht, hb = h1ring.next()
        for c in range(8):
            R.op("act", lambda h, c=c: h.activation(ht[:, c, :], pst_[:, c * 128:(c + 1) * 128], AF.Identity, bias=lnp[:, 3, c:c + 1], scale=lnp[:, 2, c:c + 1]),
                 reads=[PB[c0 + c // 4], Bc], writes=[hb])
        R.dma(pool, h1T_d[:, :, 1 + t * 128:1 + (t + 1) * 128], ht, reads=[hb], writes=[Bh1T[t]])

    def p3a_C2c(t, ctx):
        rt, rb, mv, rstd, nmr, tb = ctx
        R.op("dve", lambda h: h.tensor_tensor(rt, rt, vecs[:, 2, :], ALU.mult), reads=[rb, Bvec], writes=[rb])
        R.op("dve", lambda h: h.tensor_tensor(rt, rt, vecs[:, 3, :], ALU.add), reads=[rb, Bvec], writes=[rb])
        R.dma(pool, h1s_d[t * 128:(t + 1) * 128, :], rt, reads=[rb], writes=[Bh1s[t]])

    aA = {}
    zZ = {}
    cC = {}
    for t in range(min(3, NOWN)):
        aA[t] = p3a_A1(t)
    p3a_A2(aA[0])
    p3a_A2(aA[1])
    zZ[0] = p3a_Z(0)
    zZ[1] = p3a_Z(1)
    cC[0] = p3a_C1(0, aA[0][0], aA[0][1], zZ[0][0], zZ[0][1])
    for t in range(NOWN):
        for _ in range(2):
            if wq:
                wq.pop(0)()
        p3a_C2a(cC[t])
        if t + 2 < NOWN:
            p3a_A2(aA[t + 2])
            zZ[t + 2] = p3a_Z(t + 2)
        p3a_C2b(t, cC[t])
        if t + 1 < NOWN:
            cC[t + 1] = p3a_C1(t + 1, aA[t + 1][0], aA[t + 1][1], zZ[t + 1][0], zZ[t + 1][1])
        if t + 3 < NOWN:
            aA[t + 3] = p3a_A1(t + 3)
        p3a_C2c(t, cC[t])
    while wq:
        wq.pop(0)()

    R.barrier()

    for c in range(4, 8):
        for hf in range(2):
            R.dma(sp, w_up[:, c, hf * DFF:(hf + 1) * DFF], wup_b[c * 128:(c + 1) * 128, hf * DFF:(hf + 1) * DFF], reads=[Bs_wup[c]], writes=[Bwu[c]])
    o = 135168
    GW = 258
    hg = [(sb(o + i * 4224, (8, GW), BF16), Buf("hg%d" % i)) for i in range(2)]
    o += 8448
    gTs = [(sb(o + i * 11264, (NJ, 256), BF16), [Buf("gT%d_%d" % (i, j)) for j in range(NJ)]) for i in range(2)]
    o += 22528
    h1r = [(sb(o + i * 4096, (D,), F32), Buf("h1r%d" % i)) for i in range(2)]
    o += 8192
    fo = [(sb(o + i * 4096, (D,), F32), Buf("fo%d" % i)) for i in range(2)]
    o += 8192
    vec2 = sb(o, (2, D), F32)
    o += 8192
    cw = sb(o, (2 * NJ, 4), F32)
    o += 704
    cts = []
    for i in range(4):
        cts.append((sb(o, (256,), F32), sb(o + 1024, (256,), F32), Buf("cg%d" % i), Buf("cv%d" % i)))
        o += 2048
    assert o <= CST - 256, o
    sts4 = sts3
    Bv2 = Buf("vec2")
    R.dma(sp, vec2.rearrange("p a b -> p (a b)"), vecs_d[4:6, :].rearrange("a b -> (a b)").partition_broadcast(128), writes=[Bv2])
    R.dma(sp, cw, cw_d, writes=[Bv2])
    o2 = o
    sts4 = []
    for i in range(4):
        sts4.append((sb(o2, (12,), F32), sb(o2 + 48, (2,), F32), sb(o2 + 56, (1,), F32), sb(o2 + 60, (1,), F32), Buf("st4_%d" % i)))
        o2 += 64
    assert o2 <= CST, o2

    hgring = Ring(hg)
    gTring = Ring(gTs)
    h1rring = Ring(h1r)
    foring = Ring(fo)
    ctring = Ring(cts)
    st4ring = Ring(sts4)
    abanks = Ring([(0, 1), (2, 3)])
    fbanks = Ring([(4, 2), (6, 2)])
    Bout = Buf("out")
    ngroups = NQ // 256

    def p3b_group(g):
        t0 = g * 256
        hgt, hgb = hgring.next()
        tl = [2 * g, 2 * g + 1, min(2 * g + 2, NOWN - 1)]
        if g == 0:
            R.dma(pool, hgt[:, :, 1:GW], h1T_d[:, :, 1:GW], reads=[Bh1T[i] for i in tl], writes=[hgb])
        else:
            R.dma(pool, hgt, h1T_d[:, :, t0:t0 + GW], reads=[Bh1T[i] for i in tl], writes=[hgb])
        if g == 0:
            R.op("dve", lambda h: h.memset(hgt[:, :, 0:1], 0.0), writes=[hgb])
        gT, gTb = gTring.next()
        tail = None
        for j in range(NJ):
            if j in (2, 5, 8, 11) and pending_ln2:
                pending_ln2.pop(0)()
            t2 = p3b_j(g, j, hgt, hgb, gT, gTb)
            if tail is not None:
                tail()
            tail = t2
        tail()
        p3b_down(g, gT, gTb)

    def p3b_j(g, j, hgt, hgb, gT, gTb):
        bg, bv = abanks.next()
        psg = bank(bg)[:, 0:GW]
        psv = bank(bv)[:, 0:GW]
        for c in range(8):
            R.op("pe", lambda h, c=c: h.matmul(psg, w_up[:, c, j * 128:(j + 1) * 128], hgt[:, c, :], start=(c == 0), stop=(c == 7)),
                 reads=[Bwu[c], hgb], writes=[PB[bg]], pe_accum=(c > 0))
        for c in range(8):
            R.op("pe", lambda h, c=c: h.matmul(psv, w_up[:, c, DFF + j * 128:DFF + (j + 1) * 128], hgt[:, c, :], start=(c == 0), stop=(c == 7)),
                 reads=[Bwu[c], hgb], writes=[PB[bv]], pe_accum=(c > 0))
        cg, cv, cgb, cvb = ctring.next()
        for (ps_, dst, jj, pb, db) in ((psg, cg, j, bg, cgb), (psv, cv, NJ + j, bv, cvb)):
            R.op("act", lambda h, ps_=ps_, dst=dst, jj=jj: h.activation(dst, ps_[:, 0:256], AF.Identity, bias=cw[:, jj, 3:4], scale=cw[:, jj, 0:1]),
                 reads=[PB[pb], Bv2], writes=[db])
        for (ps_, dst, jj, pb, db) in ((psg, cg, j, bg, cgb), (psv, cv, NJ + j, bv, cvb)):
            R.op("dve", lambda h, ps_=ps_, dst=dst, jj=jj: h.scalar_tensor_tensor(dst, ps_[:, 1:257], cw[:, jj, 1:2], dst, ALU.mult, ALU.add),
                 reads=[PB[pb], Bv2, db], writes=[db])
            R.op("dve", lambda h, ps_=ps_, dst=dst, jj=jj: h.scalar_tensor_tensor(dst, ps_[:, 2:258], cw[:, jj, 2:3], dst, ALU.mult, ALU.add),
                 reads=[PB[pb], Bv2, db], writes=[db])

        def tail():
            R.op("act", lambda h: h.activation(cg, cg, AF.Gelu), reads=[cgb], writes=[cgb])
            R.op("pool", lambda h: h.tensor_tensor(gT[:, j, :], cg, cv, ALU.mult), reads=[cgb, cvb], writes=[gTb[j]])
        return tail

    def p3b_down(g, gT, gTb):
        for i in range(2):
            p3b_down_tile(g, i, gT, gTb)

    def p3b_down_tile(g, i, gT, gTb):
        if True:
            t = 2 * g + i
            b0, _ = fbanks.next()
            psf = bank(b0, 2)
            for hf in range(2):
                for j in range(NJ):
                    R.op("pe", lambda h, j=j, hf=hf, psf=psf, gT=gT, i=i: h.matmul(psf[:, hf * 512:(hf + 1) * 512], gT[:, j, i * 128:(i + 1) * 128],
                                                                                    w_down[:, j, hf * 512:(hf + 1) * 512], start=(j == 0), stop=(j == NJ - 1)),
                         reads=[gTb[j], Bwd], writes=[PB[b0 + hf]], pe_accum=(j > 0))
            hr, hrb = h1rring.next()
            R.dma(pool, hr, h1s_d[t * 128:(t + 1) * 128, :], reads=[Bh1s[t]], writes=[hrb])
            parts = p3b_ln2(t, b0, psf, hr, hrb)
            parts[0]()
            pending_ln2.extend(parts[1:])

    def p3b_ln2(t, b0, psf, hr, hrb):
        ft, fb = foring.next()
        st, mv, rstd, nmr, tb = st4ring.next()

        def part_a():
            R.op("dve", lambda h: h.scalar_tensor_tensor(ft, hr, ALPHA, psf, ALU.mult, ALU.add), reads=[hrb, PB[b0], PB[b0 + 1]], writes=[fb])
            for i in range(2):
                R.op("dve", lambda h, i=i: h.bn_stats(st[:, i * 6:(i + 1) * 6], ft[:, i * 512:(i + 1) * 512]), reads=[fb], writes=[tb])
            R.op("dve", lambda h: h.bn_aggr(mv, st[:, 0:12]), reads=[tb], writes=[tb])

        def part_b():
            R.op("act", lambda h: h.activation(rstd, mv[:, 1:2], AF.Sqrt, bias=epsc, scale=1.0), reads=[tb, Bc], writes=[tb])
            R.op("dve", lambda h: h.reciprocal(rstd, rstd), reads=[tb], writes=[tb])
            R.op("dve", lambda h: h.scalar_tensor_tensor(nmr, mv[:, 0:1], -1.0, rstd, ALU.mult, ALU.mult), reads=[tb], writes=[tb])

        def part_c():
            R.op("pool", lambda h: h.tensor_scalar(ft, ft, rstd, nmr, ALU.mult, ALU.add), reads=[fb, tb], writes=[fb])
            R.op("pool", lambda h: h.tensor_tensor(ft, ft, vec2[:, 0, :], ALU.mult), reads=[fb, Bv2], writes=[fb])
            R.op("pool", lambda h: h.tensor_tensor(ft, ft, vec2[:, 1, :], ALU.add), reads=[fb, Bv2], writes=[fb])
            R.dma(sp, out_d[t * 128:(t + 1) * 128, :], ft, reads=[fb], writes=[Bout], key="d_out")
        return [part_a, part_b, part_c]

    pending_ln2 = []
    for g in range(ngroups):
        p3b_group(g)
    while pending_ln2:
        pending_ln2.pop(0)()

    R.finish()
    return nc


def _bf16(a):
    import ml_dtypes
    return np.asarray(a, dtype=np.float32).astype(ml_dtypes.bfloat16)


def _pos_consts():
    npq = NOWN * 128
    pos = np.zeros((128, T + 2 * npq), np.float32)
    kp = np.arange(T)
    qp = np.arange(npq)
    for base in (0, 64):
        pos[base + 0, 0:T] = kp // 128
        pos[base + 1, 0:T] = kp % 128
        pos[base + 2, 0:T] = 1.0
        pos[base + 3, 0:T] = 1.0
        L = np.stack([np.full(npq, 128.0), np.ones(npq), -128.0 * (qp // 128), -1.0 * (qp % 128)])
        pos[base:base + 4, T:T + npq] = L
        pos[base:base + 4, T + npq:T + 2 * npq] = -L
    dg = np.zeros((128, 4, 128), np.float32)
    p = np.arange(128)
    k = np.arange(128)
    for a in range(2):
        kk = (p % 64) + 64 * a
        dg[p, a, kk] = 1.0
        dg[:, 2 + a, :] = -np.abs(k[None, :] - kk[:, None])
    return _bf16(pos), _bf16(dg)


def _selc():
    sel = np.zeros((128, 2, 128), np.float32)
    sel[0, 0, :] = 1.0
    sel[64, 0, :] = 1.0
    sel[32, 1, :] = 1.0
    sel[96, 1, :] = 1.0
    return sel


_NC_CACHE = {}


def kernel(x, mem, ln_emb_g, ln_emb_b, w_in, gmlp_ln_g, gmlp_ln_b, gmlp_ws, gmlp_bs,
           lambda_q1, lambda_k1, lambda_q2, lambda_k2, da_subln_g, mem_ln_g, mem_ln_b,
           w_mem_kv, w_out, ln1_g, ln1_b, w_up, conv_w, conv_b, w_down, ln2_g, ln2_b):
    f = lambda a: np.ascontiguousarray(np.asarray(a, dtype=np.float32))
    x = f(x); mem = f(mem)
    if "nc" not in _NC_CACHE:
        _NC_CACHE["nc"] = build_program()
    nc = _NC_CACHE["nc"]
    pos, dg = _pos_consts()
    fm = lambda v: f(v).reshape(8, 128).T
    lnp = np.stack([fm(ln_emb_g), fm(ln_emb_b), fm(ln1_g[0]), fm(ln1_b[0]), fm(mem_ln_g[0]), fm(mem_ln_b[0])], axis=1)
    vecs = np.stack([f(ln_emb_g), f(ln_emb_b), f(ln1_g[0]), f(ln1_b[0]), f(ln2_g[0]), f(ln2_b[0]), f(ln2_g[0]), f(ln2_b[0])])
    gml = np.stack([f(gmlp_ln_g[0]), f(gmlp_ln_b[0])])
    lam = np.stack([f(lambda_q1[0]), f(lambda_k1[0]), f(lambda_q2[0]), f(lambda_k2[0])])
    subg = f(da_subln_g[0]).reshape(128, 1)
    shared = {
        "w_in": f(w_in[0]), "w_mem_kv": f(w_mem_kv[0]), "w_out": f(w_out[0]), "w_up": f(w_up[0]), "w_down": f(w_down[0]),
        "lnp": np.ascontiguousarray(lnp), "vecs": np.ascontiguousarray(vecs), "gml": gml, "lam": lam, "subg": subg,
        "pos": pos, "dg": dg, "identf": np.eye(128, dtype=np.float32), "selc": _selc(),
    }
    ws = f(gmlp_ws[0]); bs = f(gmlp_bs[0]); cwt = f(conv_w[0]); cb = f(conv_b[0])
    in_maps = []
    for c in range(8):
        b, half = c // 2, c % 2
        if half == 0:
            xl = x[b]; wsl = ws; bsl = bs; cwl = cwt
        else:
            xl = x[b][::-1]; wsl = ws[:, ::-1, ::-1]; bsl = bs[:, ::-1]; cwl = cwt[::-1]
        cw4 = np.concatenate([cwl, cb[None, :]], axis=0)
        m = dict(shared)
        m["x"] = np.ascontiguousarray(xl)
        m["mem"] = mem[b]
        m["wsT"] = np.ascontiguousarray(wsl.transpose(2, 0, 1))
        m["bsT"] = np.ascontiguousarray(bsl.T)
        m["cw"] = np.ascontiguousarray(cw4.T.reshape(2 * NJ, 128, 4).transpose(1, 0, 2))
        in_maps.append(m)
    res = run_bass_kernel_spmd(nc, in_maps, core_ids=list(range(8)))
    out = np.empty((4, T, D), np.float32)
    for c in range(8):
        b, half = c // 2, c % 2
        o = res.results[c]["out"]
        if half == 0:
            out[b, 0:NQ] = o
        else:
            out[b, NQ:T] = o[::-1]
    kernel.last_results = res.results
    return out
```

```python
import numpy as np
import concourse.bass as bass
import concourse.mybir as mybir
from concourse.bass_utils import run_bass_kernel_spmd

F32 = mybir.dt.float32
BF16 = mybir.dt.bfloat16
U8 = mybir.dt.uint8
AF = mybir.ActivationFunctionType
ALU = mybir.AluOpType

D = 1024
T = 4096
NQ = 2048
NOWN = 17
NKT = 32
DFF = 2816
NJ = 22
INW = 2304
ALPHA = 2.0 ** 0.25
EPS = 1e-5
LAM_INIT = 0.2
SBUF_BYTES = 212480
DEBUG = False


class Buf:
    __slots__ = ("name", "last_w", "readers")

    def __init__(self, name):
        self.name = name
        self.last_w = None
        self.readers = []


class Rec:
    ENGS = ("pe", "act", "dve", "pool", "sp")

    def __init__(self, nc):
        self.nc = nc
        self.prog = {e: [] for e in self.ENGS}
        self.count = {e: 0 for e in self.ENGS}
        self.seen = {e: {} for e in self.ENGS}
        self.sems = {}
        self.dma_total = {}
        self.dma_keys = 0

    def sem(self, key):
        if key not in self.sems:
            self.sems[key] = self.nc.alloc_semaphore("s_%s" % (key,))
        return self.sems[key]

    def _deps(self, e, reads, writes, pe_accum=False):
        need = {}

        def add(t):
            if t is None:
                return
            k, v = t
            if need.get(k, 0) < v:
                need[k] = v
        for b in reads:
            add(b.last_w)
        for b in writes:
            if not (pe_accum and b.last_w is not None and b.last_w[0] == "pe" and not b.readers):
                add(b.last_w)
            for t in b.readers:
                add(t)
        waits = []
        for k, v in need.items():
            if k == e and e == "pe":
                continue
            if self.seen[e].get(k, 0) >= v:
                continue
            self.seen[e][k] = v
            waits.append((self.sem(k), v))
        return waits

    def op(self, e, fn, reads=(), writes=(), pe_accum=False, inc=True):
        waits = self._deps(e, reads, writes, pe_accum)
        if not inc:
            t = (e, self.count[e] + 1)
            for b in reads:
                b.readers.append(t)
            for b in writes:
                b.last_w = t
                b.readers = []

            def emit0(h, waits=waits, fn=fn):
                for s_, v in waits:
                    h.wait_ge(s_, v)
                fn(h)
            self.prog[e].append(emit0)
            return t
        self.count[e] += 1
        t = (e, self.count[e])
        sem = self.sem(e)
        for b in reads:
            b.readers.append(t)
        for b in writes:
            b.last_w = t
            b.readers = []

        def emit(h, waits=waits, fn=fn, sem=sem):
            for s, v in waits:
                h.wait_ge(s, v)
            fn(h).then_inc(sem, 1)
        self.prog[e].append(emit)
        return t

    def dma(self, e, out, in_, reads=(), writes=(), key=None):
        waits = self._deps(e, reads, writes)
        if key is None:
            key = "d_" + (writes[0].name if writes else reads[0].name)
        key = key + "_" + e
        sem = self.sem(key)
        self.dma_total[key] = self.dma_total.get(key, 0) + 16
        t = (key, self.dma_total[key])
        for b in reads:
            b.readers.append(t)
        for b in writes:
            b.last_w = t
            b.readers = []

        def emit(h, waits=waits, sem=sem, out=out, in_=in_):
            for s, v in waits:
                h.wait_ge(s, v)
            h.dma_start(out=out, in_=in_).then_inc(sem, 16)
        self.prog[e].append(emit)
        return t

    def barrier(self):
        targets = [(e, self.count[e]) for e in self.ENGS if self.count[e] > 0]
        targets += list(self.dma_total.items())
        for e in self.ENGS:
            waits = []
            for k, v in targets:
                if k == e or self.seen[e].get(k, 0) >= v:
                    continue
                self.seen[e][k] = v
                waits.append((self.sem(k), v))
            if waits:
                def emit(h, waits=waits):
                    for s, v in waits:
                        h.wait_ge(s, v)
                self.prog[e].append(emit)

    def finish(self):
        self.barrier()
        nc = self.nc
        with nc.Block() as block:
            @block.tensor
            def _(h):
                for f in self.prog["pe"]:
                    f(h)

            @block.scalar
            def _(h):
                for f in self.prog["act"]:
                    f(h)

            @block.vector
            def _(h):
                for f in self.prog["dve"]:
                    f(h)

            @block.gpsimd
            def _(h):
                for f in self.prog["pool"]:
                    f(h)

            @block.sync
            def _(h):
                for f in self.prog["sp"]:
                    f(h)


class Ring:
    def __init__(self, items):
        self.items = items
        self.i = 0

    def next(self):
        it = self.items[self.i % len(self.items)]
        self.i += 1
        return it


def build_program():
    nc = bass.Bass("TRN2", target_bir_lowering=False)
    R = Rec(nc)

    def din(name, shape, dt=F32):
        return nc.dram_tensor(name, list(shape), dt, kind="ExternalInput").ap()

    x_d = din("x", [T, D])
    mem_d = din("mem", [256, D])
    w_in_d = din("w_in", [D, INW])
    w_mem_d = din("w_mem_kv", [D, 512])
    w_out_d = din("w_out", [D, D])
    w_up_d = din("w_up", [D, 2 * DFF])
    w_down_d = din("w_down", [DFF, D])
    wsT_d = din("wsT", [128, 4, 128])
    bsT_d = din("bsT", [128, 4])
    cw_d = din("cw", [128, 2 * NJ, 4])
    lnp_d = din("lnp", [128, 6, 8])
    vecs_d = din("vecs", [8, D])
    gml_d = din("gml", [2, 256])
    lam_d = din("lam", [4, 64])
    subg_d = din("subg", [128, 1])
    pos_d = din("pos", [128, T + 2 * (NOWN * 128)], BF16)
    dg_d = din("dg", [128, 4, 128], BF16)
    identf_d = din("identf", [128, 128])
    selc_d = din("selc", [128, 2, 128])
    out_d = nc.dram_tensor("out", [NQ, D], F32, kind="ExternalOutput").ap()
    h1s_d = nc.dram_tensor("h1s", [NOWN * 128, D], F32, kind="Internal").ap()
    h1T_d = nc.dram_tensor("h1Ts", [128, 8, 1 + NOWN * 128], BF16, kind="Internal").ap()
    wup_b = nc.dram_tensor("wup_b", [D, 2 * DFF], BF16, kind="Internal").ap()
    wdn_b = nc.dram_tensor("wdn_b", [DFF, D], BF16, kind="Internal").ap()
    wout_b = nc.dram_tensor("wout_b", [D, D], BF16, kind="Internal").ap()
    win_b = nc.dram_tensor("win_b", [D, INW], BF16, kind="Internal").ap()
    if DEBUG:
        dbg_d = nc.dram_tensor("dbg", [128, 8, 1 + NOWN * 128], BF16, kind="ExternalOutput").ap()

    big = nc.alloc_sbuf_tensor("big", [128, SBUF_BYTES], U8)
    bap = big.ap()
    pst = nc.alloc_psum_tensor("pst", [128, 4096], F32)
    psap = pst.ap()

    def sb(off, free, dt):
        esz = 4 if dt == F32 else 2
        n = int(np.prod(free))
        assert off + n * esz <= SBUF_BYTES, (off, n, esz)
        v = bap[:, off:off + n * esz].bitcast(dt)
        if len(free) == 2:
            v = v.rearrange("p (a b) -> p a b", a=free[0])
        elif len(free) == 3:
            v = v.rearrange("p (a b c) -> p a b c", a=free[0], b=free[1])
        return v

    def bank(i, n=1):
        return psap[:, i * 512:(i + n) * 512]

    PB = [Buf("ps%d" % i) for i in range(8)]

    KT = sb(0, (4, T), BF16)
    KTb = [Buf("KT%d" % i) for i in range(8)]
    Vt = sb(32768, (NKT, 512), BF16)
    Vb = [Buf("V%d" % i) for i in range(NKT)]
    QT = sb(65536, (4, NOWN * 128), BF16)
    QTb = [Buf("QT%d" % i) for i in range(5)]
    QmT = sb(82944, (2, NOWN * 128), BF16)
    QmTb = [Buf("QmT%d" % i) for i in range(5)]
    YW = 1 + NOWN * 128
    yT = sb(91648, (8, YW), BF16)
    yTb = [Buf("yT%d" % i) for i in range(NOWN)]
    P0 = 126976
    CST = SBUF_BYTES - 4096
    identf = sb(CST, (128,), F32)
    lnp = sb(CST + 512, (6, 8), F32)
    epsc = sb(CST + 768, (1,), F32)
    nlam = sb(CST + 772, (1,), F32)
    subg = sb(CST + 776, (1,), F32)
    lamt = sb(CST + 1024, (4, 64), F32)
    lamr = sb(CST + 2048, (4,), F32)
    bsT = sb(CST + 2064, (4,), F32)
    onesb = sb(CST + 2304, (128,), BF16)
    onesf = sb(CST + 2560, (128,), F32)
    zcol = sb(CST + 3072, (8, 1), BF16)
    prs = sb(CST + 3200, (20,), F32)
    pnm = sb(CST + 3280, (20,), F32)
    Bprs = Buf("prs")
    selm = sb(CST + 1024, (2, 128), F32)
    Bc = Buf("consts")

    sp, pool = "sp", "pool"

    R.op("dve", lambda h: h.memset(epsc, EPS), writes=[Bc])
    R.op("dve", lambda h: h.memset(onesb, 1.0), writes=[Bc])
    R.op("dve", lambda h: h.memset(onesf, 1.0), writes=[Bc])
    R.op("dve", lambda h: h.memset(zcol, 0.0), writes=[Bc])
    R.dma("act", identf, identf_d, writes=[Bc])
    R.dma("act", lnp, lnp_d, writes=[Bc])
    R.dma("act", bsT, bsT_d, writes=[Bc])
    Bl = Buf("lamc")

    def late_consts():
        R.dma("act", subg, subg_d, writes=[Bl])
        R.dma("act", lamt.rearrange("p a b -> p (a b)"), lam_d.rearrange("a b -> (a b)").partition_broadcast(128), writes=[Bl])
        R.op("dve", lambda h: h.tensor_tensor(lamt[:, 0, :], lamt[:, 0, :], lamt[:, 1, :], ALU.mult), reads=[Bl], writes=[Bl])
        R.op("dve", lambda h: h.tensor_tensor(lamt[:, 2, :], lamt[:, 2, :], lamt[:, 3, :], ALU.mult), reads=[Bl], writes=[Bl])
        R.op("dve", lambda h: h.reduce_sum(lamr[:, 0:1], lamt[:, 0, :], mybir.AxisListType.X), reads=[Bl], writes=[Bl])
        R.op("dve", lambda h: h.reduce_sum(lamr[:, 1:2], lamt[:, 2, :], mybir.AxisListType.X), reads=[Bl], writes=[Bl])
        R.op("act", lambda h: h.activation(lamr[:, 2:4], lamr[:, 0:2], AF.Exp), reads=[Bl], writes=[Bl])
        R.op("dve", lambda h: h.tensor_tensor(nlam, lamr[:, 3:4], lamr[:, 2:3], ALU.subtract), reads=[Bl], writes=[Bl])
        R.op("dve", lambda h: h.tensor_scalar_add(nlam, nlam, -LAM_INIT), reads=[Bl], writes=[Bl])
        R.op("dve", lambda h: h.tensor_scalar_mul(subg, subg, 1.0 - LAM_INIT), reads=[Bl], writes=[Bl])
        R.dma("act", selm, selc_d, reads=[Bl], writes=[Bl])


    def ln_stats(src, srcb, rstd, nmr, st, mv, tb, n=D):
        nch = (n + 511) // 512
        w = n // nch
        for i in range(nch):
            R.op("dve", lambda h, i=i: h.bn_stats(st[:, i * 6:(i + 1) * 6], src[:, i * w:(i + 1) * w]), reads=[srcb], writes=[tb])
        R.op("dve", lambda h: h.bn_aggr(mv, st[:, 0:6 * nch]), reads=[tb], writes=[tb])
        R.op("act", lambda h: h.activation(rstd, mv[:, 1:2], AF.Sqrt, bias=epsc, scale=1.0), reads=[tb, Bc], writes=[tb])
        R.op("dve", lambda h: h.reciprocal(rstd, rstd), reads=[tb], writes=[tb])
        R.op("dve", lambda h: h.scalar_tensor_tensor(nmr, mv[:, 0:1], -1.0, rstd, ALU.mult, ALU.mult), reads=[tb], writes=[tb])

    def dve_rstd(var_ap, mean_ap, rstd, nmr, t1, t2, tb, iters=6):
        R.op("act", lambda h: h.activation(rstd, var_ap, AF.Sqrt, bias=epsc, scale=1.0), reads=[tb, Bc], writes=[tb])
        R.op("dve", lambda h: h.reciprocal(rstd, rstd), reads=[tb], writes=[tb])
        R.op("dve", lambda h: h.scalar_tensor_tensor(nmr, mean_ap, -1.0, rstd, ALU.mult, ALU.mult), reads=[tb], writes=[tb])

    w_in = sb(P0, (8, INW), BF16)
    Bw_in = Buf("w_in")
    o = P0 + 36864
    xts = [(sb(o + i * 4096, (D,), F32), Buf("xt%d" % i)) for i in range(4)]
    o += 16384
    YT_FREE = 91648 + 2 * YW * 2
    assert YT_FREE + 4 * 4096 <= 126976
    xts += [(sb(YT_FREE + i * 4096, (D,), F32), Buf("xt%d" % (4 + i))) for i in range(4)]
    hTs = [(sb(o + i * 8192, (8, 512), BF16), [Buf("hT%d_%d" % (i, k)) for k in range(4)]) for i in range(2)]
    o += 16384
    gm = []
    for i in range(2):
        gu = sb(o, (256,), F32)
        gv = sb(o + 1024, (256,), F32)
        yg = sb(o + 2048, (256,), F32)
        vln = sb(o + 3072, (256,), BF16)
        gm.append((gu, gv, yg, vln, Buf("gmu%d" % i), Buf("gmv%d" % i), Buf("gmy%d" % i), Buf("gml%d" % i), sb(o, (512,), F32)))
        o += 3584
    gml = sb(o, (2, 256), F32)
    o += 2048
    wsT = sb(o, (4, 128), BF16)
    o += 1024
    sts = []
    for i in range(8):
        sts.append((sb(o, (12,), F32), sb(o + 48, (2,), F32), sb(o + 56, (1,), F32), sb(o + 60, (1,), F32), Buf("st%d" % i)))
        o += 64
    gsts = []
    og = YT_FREE + 4 * 4096
    for i in range(6):
        gsts.append(dict(st=sb(og, (4, 12), F32), mv=sb(og + 192, (4, 2), F32), rstd=sb(og + 224, (4,), F32), nmr=sb(og + 240, (4,), F32),
                         t1=sb(og + 256, (4,), F32), t2=sb(og + 272, (4,), F32), tb=Buf("gst%d" % i)))
        og += 288
    embgb = sb(og, (2, D), F32)
    og += 8192
    assert og <= 126976
    assert o <= CST, o
    Bp1 = Buf("p1c")
    gstring = Ring(gsts)

    stg = [(sb(i * 9216, (INW,), F32), Buf("stg%d" % i)) for i in range(3)]
    Bw_c = [Buf("w_in_c%d" % c) for c in range(8)]

    def load_w_in_chunk(c):
        st_, stb = stg[c % 3]
        R.dma(sp, st_, w_in_d[c * 128:(c + 1) * 128, :], writes=[stb])
        if c % 2 == 0:
            R.op("act", lambda h: h.copy(w_in[:, c, :], st_), reads=[stb], writes=[Bw_c[c]])
        else:
            R.op("dve", lambda h: h.tensor_copy(w_in[:, c, :], st_), reads=[stb], writes=[Bw_c[c]])

    R.dma(pool, wsT, wsT_d, writes=[Bp1])
    R.dma("act", embgb.rearrange("p a b -> p (a b)"), vecs_d[0:2, :].rearrange("a b -> (a b)").partition_broadcast(128), writes=[Bp1])
    R.dma("act", gml.rearrange("p a b -> p (a b)"), gml_d.rearrange("a b -> (a b)").partition_broadcast(128), writes=[Bp1])

    xring = Ring(xts)
    string = Ring(sts)
    gmring = Ring(gm)
    tp_banks = Ring([(0, 2), (2, 2)])
    fm_banks = Ring([4, 5])
    tm_banks = Ring([6, 7])

    def p1_ln_a(t):
        xt, xb = xring.next()
        R.dma(sp, xt, x_d[t * 128:(t + 1) * 128, :], writes=[xb])
        st, mv, rstd, nmr, tb = string.next()
        ln_stats(xt, xb, rstd, nmr, st, mv, tb)
        R.op("act", lambda h: h.activation(xt, xt, AF.Identity, bias=nmr, scale=rstd), reads=[xb, tb], writes=[xb])
        return xt, xb

    def p1_ln_t(xt, xb, hT, hTb_k, k):
        b0, _ = tp_banks.next()
        ps = bank(b0, 2)
        for c in range(8):
            R.op("pe", lambda h, c=c: h.transpose(ps[:, c * 128:(c + 1) * 128], xt[:, c * 128:(c + 1) * 128], identf),
                 reads=[xb, Bc], writes=[PB[b0 + c // 4]])
        for hb_ in range(2):
            src = ps[:, hb_ * 512:(hb_ + 1) * 512].rearrange("p (a b) -> p a b", a=4)
            dst = hT[:, hb_ * 4:(hb_ + 1) * 4, k * 128:(k + 1) * 128]
            if hb_ == 0:
                R.op("act", lambda h, src=src, dst=dst: h.copy(dst, src), reads=[PB[b0 + hb_]], writes=[hTb_k])
            else:
                R.op("dve", lambda h, src=src, dst=dst: h.tensor_copy(dst, src), reads=[PB[b0 + hb_]], writes=[hTb_k])

    evtog = [0]

    def fm_proj(hT, hTbs, ntok, col0, dst, dstb, scale=None):
        b = fm_banks.next()
        ps = bank(b)[:, 0:ntok]
        nk = (ntok + 127) // 128
        for c in range(8):
            R.op("pe", lambda h, c=c: h.matmul(ps, w_in[:, c, col0:col0 + 128], hT[:, c, 0:ntok], start=(c == 0), stop=(c == 7)),
                 reads=[Bw_c[c]] + hTbs[:nk], writes=[PB[b]], pe_accum=(c > 0))
        evtog[0] ^= 1
        if scale is None:
            if evtog[0]:
                R.op("act", lambda h: h.copy(dst, ps), reads=[PB[b]], writes=[dstb])
            else:
                R.op("dve", lambda h: h.tensor_copy(dst, ps), reads=[PB[b]], writes=[dstb])
        else:
            if evtog[0]:
                R.op("act", lambda h: h.mul(dst, ps, scale), reads=[PB[b]], writes=[dstb])
            else:
                R.op("dve", lambda h: h.tensor_scalar_mul(dst, ps, scale), reads=[PB[b]], writes=[dstb])

    def tm_proj(hT, hTb_k, k, col0):
        b = tm_banks.next()
        ps = bank(b)
        for c in range(8):
            R.op("pe", lambda h, c=c: h.matmul(ps, hT[:, c, k * 128:(k + 1) * 128], w_in[:, c, col0:col0 + 512], start=(c == 0), stop=(c == 7)),
                 reads=[Bw_c[c], hTb_k], writes=[PB[b]], pe_accum=(c > 0))
        return b, ps

    def v_proj(t, hT, hTb_k, k):
        b, ps = tm_proj(hT, hTb_k, k, 1536)
        evtog[0] ^= 1
        if evtog[0]:
            R.op("dve", lambda h: h.tensor_copy(Vt[:, t, :], ps), reads=[PB[b]], writes=[Vb[t]])
        else:
            R.op("act", lambda h: h.copy(Vt[:, t, :], ps), reads=[PB[b]], writes=[Vb[t]])

    def gm1(t, hT, hTb_k, k):
        b, ps = tm_proj(hT, hTb_k, k, 0)
        gu, gv, yg, vln, gub, gvb, gyb, glb, guv = gmring.next()
        R.op("act", lambda h: h.activation(guv, ps, AF.Gelu), reads=[PB[b]], writes=[gub, gvb])
        G = gmG[0]
        kk = gmG[1]
        gmG[1] += 1
        st, mv, tb = G["st"], G["mv"], G["tb"]
        R.op("dve", lambda h: h.bn_stats(st[:, kk, 0:6], gv), reads=[gvb], writes=[tb])
        R.op("dve", lambda h: h.bn_aggr(mv[:, kk, :], st[:, kk, 0:6]), reads=[tb], writes=[tb])
        R.op("dve", lambda h: h.scalar_tensor_tensor(gv, gv, mv[:, kk, 0:1], gml[:, 0, :], ALU.subtract, ALU.mult), reads=[gvb, tb, Bp1], writes=[gvb])
        return (t, gu, gv, yg, vln, gub, gvb, gyb, glb, G, kk, None, tb)

    gmG = [None, 0]

    def gm_begin():
        gmG[0] = gstring.next()
        gmG[1] = 0

    def gm_rstd(n):
        G = gmG[0]
        dve_rstd(G["mv"][:, 0:n, 1], G["mv"][:, 0:n, 0], G["rstd"][:, 0:n], G["nmr"][:, 0:n], G["t1"][:, 0:n], G["t2"][:, 0:n], G["tb"])

    def gm2(ctx):
        (t, gu, gv, yg, vln, gub, gvb, gyb, glb, G, kk, _, tb) = ctx
        R.op("dve", lambda h: h.scalar_tensor_tensor(vln, gv, G["rstd"][:, kk:kk + 1], gml[:, 1, :], ALU.mult, ALU.add), reads=[gvb, tb, Bp1], writes=[glb])

    def gm3(ctx):
        (t, gu, gv, yg, vln, gub, gvb, gyb, glb, G, kk, _, tb) = ctx
        sbk = tm_banks.next()
        pss = bank(sbk)
        for g in range(4):
            R.op("pe", lambda h, g=g: h.matmul(pss[:, g * 64:(g + 1) * 64], wsT[:, g, :], vln[:, g * 64:(g + 1) * 64], start=True, stop=True),
                 reads=[glb, Bp1], writes=[PB[sbk]])
        for g in range(4):
            R.op("dve", lambda h, g=g: h.scalar_tensor_tensor(yg[:, g * 64:(g + 1) * 64], pss[:, g * 64:(g + 1) * 64], bsT[:, g:g + 1],
                                                              gu[:, g * 64:(g + 1) * 64], ALU.add, ALU.mult),
                 reads=[PB[sbk], gub, Bc], writes=[gyb])
        for c in range(2):
            R.op("pe", lambda h, c=c: h.transpose(pss[:, 256 + c * 128:256 + (c + 1) * 128], yg[:, c * 128:(c + 1) * 128], identf),
                 reads=[gyb, Bc], writes=[PB[sbk]])
        R.op("dve", lambda h: h.tensor_copy(yT[:, 0:2, 1 + t * 128:1 + (t + 1) * 128], pss[:, 256:512].rearrange("p (a b) -> p a b", a=2)),
             reads=[PB[sbk]], writes=[yTb[t]])

    hring = Ring(hTs)
    groups = [list(range(g * 4, g * 4 + 4)) for g in range(8)]
    lnA = {}
    lnG = {}

    def do_ln_a(g):
        G = gstring.next()
        st, mv, tb = G["st"], G["mv"], G["tb"]
        res = []
        for k, t in enumerate(groups[g]):
            xt, xb = xring.next()
            R.dma(sp, xt, x_d[t * 128:(t + 1) * 128, :], writes=[xb])
            for i in range(2):
                R.op("dve", lambda h, i=i, k=k, xt=xt: h.bn_stats(st[:, k, i * 6:(i + 1) * 6], xt[:, i * 512:(i + 1) * 512]), reads=[xb], writes=[tb])
            R.op("dve", lambda h, k=k: h.bn_aggr(mv[:, k, :], st[:, k, :]), reads=[tb], writes=[tb])
            res.append((xt, xb))
        lnA[g] = res
        lnG[g] = G

    def do_ln_b(g):
        G = lnG[g]
        mv, tb = G["mv"], G["tb"]
        res = lnA[g]
        dve_rstd(mv[:, :, 1], mv[:, :, 0], G["rstd"], G["nmr"], G["t1"], G["t2"], tb)
        for k in range(4):
            xt, xb = res[k]
            R.op("dve", lambda h, k=k, xt=xt: h.tensor_scalar(xt, xt, G["rstd"][:, k:k + 1], G["nmr"][:, k:k + 1], ALU.mult, ALU.add),
                 reads=[xb, tb], writes=[xb])
            R.op("pool", lambda h, xt=xt: h.tensor_tensor(xt, xt, embgb[:, 0, :], ALU.mult), reads=[xb, Bp1], writes=[xb])
            R.op("pool", lambda h, xt=xt: h.tensor_tensor(xt, xt, embgb[:, 1, :], ALU.add), reads=[xb, Bp1], writes=[xb])
        if g <= 4:
            R.op("dve", lambda h: h.tensor_copy(prs[:, g * 4:(g + 1) * 4], G["rstd"]), reads=[tb], writes=[Bprs])
            R.op("dve", lambda h: h.tensor_copy(pnm[:, g * 4:(g + 1) * 4], G["nmr"]), reads=[tb], writes=[Bprs])

    def do_ln_t(g):
        hT, hTbs = hring.next()
        for k in range(4):
            xt, xb = lnA[g][k]
            p1_ln_t(xt, xb, hT, hTbs[k], k)
        return hT, hTbs

    def proj_pieces(g, hT, hTbs):
        tiles = groups[g]
        t0 = tiles[0] * 128
        own = g <= 4
        ntok = 512 if g < 4 else 128
        nfull = 4 if g < 4 else (1 if g == 4 else 0)
        P = []
        st = {}

        def mk_gm1(k, key):
            def f():
                if key not in st:
                    gm_begin()
                    st[key] = []
                st[key].append(gm1(tiles[k], hT, hTbs[k], k))
            return f

        def mk_gm2(key):
            def f():
                if st.get(key):
                    gm_rstd(len(st[key]))
                    for c_ in st[key]:
                        gm2(c_)
            return f

        def mk_gm3(key, i):
            def f():
                if st.get(key) and i < len(st[key]):
                    gm3(st[key][i])
            return f

        if own:
            for k in range(min(2, nfull)):
                P.append(mk_gm1(k, "a"))
        for hh in range(4):
            P.append(lambda hh=hh: fm_proj(hT, hTbs, 512, 1024 + hh * 128, KT[:, hh, t0:t0 + 512], KTb[g]))
        if own:
            P.append(mk_gm2("a"))
        for k, t in enumerate(tiles):
            P.append(lambda k=k, t=t: v_proj(t, hT, hTbs[k], k))
        if own:
            P.append(mk_gm3("a", 0))
            P.append(mk_gm3("a", 1))
            for k in range(2, nfull):
                P.append(mk_gm1(k, "b"))
            for hh in range(4):
                P.append(lambda hh=hh: fm_proj(hT, hTbs, ntok, 512 + hh * 128, QT[:, hh, t0:t0 + ntok], QTb[g], scale=2.0 ** (-3 + 2 * (hh + 1))))
            P.append(mk_gm2("b"))
            for cc in range(2):
                P.append(lambda cc=cc: fm_proj(hT, hTbs, ntok, 2048 + cc * 128, QmT[:, cc, t0:t0 + ntok], QmTb[g], scale=0.125))
            P.append(mk_gm3("b", 0))
            P.append(mk_gm3("b", 1))
        return P

    do_ln_a(0)
    do_ln_b(0)
    for c in range(8):
        load_w_in_chunk(c)
    do_ln_a(1)
    do_ln_b(1)
    cur = do_ln_t(0)
    for g in range(8):
        P = proj_pieces(g, cur[0], cur[1])
        n = len(P)
        nxt = None
        if g + 1 < 8:
            hTn, hTbn = hring.next()
            nxt = (hTn, hTbn)
            marks = {max(1, round((k + 1) * n / 5.0)): k for k in range(4)}
        else:
            marks = {}
        if g + 2 < 8:
            do_ln_a(g + 2)
        for i, f in enumerate(P):
            if i in marks:
                k = marks[i]
                xt, xb = lnA[g + 1][k]
                p1_ln_t(xt, xb, hTn, hTbn[k], k)
            f()
            if g + 2 < 8 and i == n // 2:
                do_ln_b(g + 2)
        cur = nxt

    late_consts()
    R.barrier()

    o = P0
    NPQ = NOWN * 128
    pos = sb(o, (T + 2 * NPQ,), BF16)
    o += (T + 2 * NPQ) * 2
    PK = pos[:, 0:T]
    PQL = pos[:, T:T + NPQ]
    PQR = pos[:, T + NPQ:T + 2 * NPQ]
    dg = sb(o, (4, 128), BF16)
    o += 1024
    Bpos = Buf("pos")
    ETall = sb(o, (4, 1024), BF16)
    ETs = [(sb(o + i * 2048, (1024,), BF16), Buf("ET%d" % i)) for i in range(4)]
    o += 8192
    yacc = [(sb(o + i * 2048, (512,), F32), Buf("yacc%d" % i)) for i in range(4)]
    o += 8192
    sqs = [(sb(o + i * 2048, (512,), F32), Buf("sq%d" % i)) for i in range(4)]
    o += 8192
    tmps = [(sb(o + i * 2048, (512,), F32), Buf("tmp%d" % i)) for i in range(4)]
    o += 8192
    w_mem = sb(o, (8, 512), BF16)
    o += 8192
    memT = sb(o, (8, 256), BF16)
    o += 4096
    KmT = sb(o, (2, 256), BF16)
    o += 1024
    VmP = sb(o, (2, 4, 128), BF16)
    o += 2048
    onesP = sb(o, (2, 128), BF16)
    o += 512
    mts = [(sb(o + i * 4096, (D,), F32), Buf("mt%d" % i)) for i in range(2)]
    o += 8192
    assert o <= CST, o
    Bmem = Buf("mem")
    BmT = Buf("memT")
    Bkm = Buf("KmT")
    Bvm = Buf("VmP")
    Bon = Buf("onesP")
    Bwm = Buf("w_mem")

    R.dma(sp, pos, pos_d, writes=[Bpos])
    R.dma(sp, dg, dg_d, writes=[Bpos])
    for c in range(8):
        R.dma(pool, w_mem[:, c, :], w_mem_d[c * 128:(c + 1) * 128, :], writes=[Bwm])
    Bs_wout = Buf("s_wout")
    Bs_wup = [Buf("s_wup%d" % c) for c in range(8)]
    Bs_wdn = Buf("s_wdn")
    for c in range(8):
        R.dma(pool, wout_b[c * 128:(c + 1) * 128, :], w_out_d[c * 128:(c + 1) * 128, :], writes=[Bs_wout])
    for j in range(NJ):
        R.dma(pool, wdn_b[j * 128:(j + 1) * 128, :], w_down_d[j * 128:(j + 1) * 128, :], writes=[Bs_wdn])
    for c in range(8):
        R.dma(pool, wup_b[c * 128:(c + 1) * 128, :], w_up_d[c * 128:(c + 1) * 128, :], writes=[Bs_wup[c]])
    R.op("dve", lambda h: h.memset(VmP, 0.0), writes=[Bvm])
    R.op("dve", lambda h: h.memset(onesP, 0.0), writes=[Bon])
    R.op("dve", lambda h: h.memset(onesP[:, 0, 0:64], 1.0), writes=[Bon])
    R.op("dve", lambda h: h.memset(onesP[:, 1, 64:128], 1.0), writes=[Bon])

    for mt in range(2):
        xt, xb = mts[mt]
        R.dma(sp, xt, mem_d[mt * 128:(mt + 1) * 128, :], writes=[xb])
        st, mv, rstd, nmr, tb = sts[mt]
        ln_stats(xt, xb, rstd, nmr, st, mv, tb)
        R.op("act", lambda h, xt=xt, nmr=nmr, rstd=rstd: h.activation(xt, xt, AF.Identity, bias=nmr, scale=rstd), reads=[xb, tb], writes=[xb])
        ps = bank(2 * mt, 2)
        for c in range(8):
            R.op("pe", lambda h, c=c, ps=ps, xt=xt: h.transpose(ps[:, c * 128:(c + 1) * 128], xt[:, c * 128:(c + 1) * 128], identf),
                 reads=[xb, Bc], writes=[PB[2 * mt + c // 4]])
        for c in range(8):
            R.op("dve", lambda h, c=c, ps=ps, mt=mt: h.tensor_scalar(memT[:, c, mt * 128:(mt + 1) * 128], ps[:, c * 128:(c + 1) * 128],
                                                                      lnp[:, 4, c:c + 1], lnp[:, 5, c:c + 1], ALU.mult, ALU.add),
                 reads=[PB[2 * mt + c // 4], Bc], writes=[BmT])
    for cc in range(2):
        ps = bank(4 + cc)[:, 0:256]
        for c in range(8):
            R.op("pe", lambda h, c=c, ps=ps, cc=cc: h.matmul(ps, w_mem[:, c, cc * 128:(cc + 1) * 128], memT[:, c, :], start=(c == 0), stop=(c == 7)),
                 reads=[Bwm, BmT], writes=[PB[4 + cc]], pe_accum=(c > 0))
        R.op("dve", lambda h, ps=ps, cc=cc: h.tensor_copy(KmT[:, cc, :], ps), reads=[PB[4 + cc]], writes=[Bkm])
    for mt in range(2):
        ps = bank(6 + mt)[:, 0:256]
        for c in range(8):
            R.op("pe", lambda h, c=c, ps=ps, mt=mt: h.matmul(ps, memT[:, c, mt * 128:(mt + 1) * 128], w_mem[:, c, 256:512], start=(c == 0), stop=(c == 7)),
                 reads=[Bwm, BmT], writes=[PB[6 + mt]], pe_accum=(c > 0))
        for hh in range(4):
            R.op("dve", lambda h, ps=ps, mt=mt, hh=hh: h.tensor_copy(VmP[:, mt, hh, (hh % 2) * 64:(hh % 2) * 64 + 64], ps[:, hh * 64:(hh + 1) * 64]),
                 reads=[PB[6 + mt]], writes=[Bvm])

    etring = Ring(ETs)
    tmpring = Ring(tmps)

    chunks = [(0, 4), (4, 4), (8, 4), (12, 4), (16, 1)]

    def mem_qk(ci, hp):
        qt0, nt = chunks[ci]
        nq = nt * 128
        q0 = qt0 * 128
        for hl in range(2):
            for mt in range(2):
                bb = hl * 2 + mt
                R.op("pe", lambda h, bb=bb, hl=hl, mt=mt: h.matmul(bank(bb)[:, 0:nq], KmT[hl * 64:(hl + 1) * 64, hp, mt * 128:(mt + 1) * 128],
                                                                   QmT[hl * 64:(hl + 1) * 64, hp, q0:q0 + nq], start=True, stop=True),
                     reads=[Bkm, QmTb[ci]], writes=[PB[bb]])

    def mem_exp(ci, hp):
        nq = chunks[ci][1] * 128
        R.op("act", lambda h: h.activation(ETall[:, :, 0:nq], bank(0, 4).rearrange("p (a b) -> p a b", a=4)[:, :, 0:nq], AF.Exp),
             reads=[PB[0], PB[1], PB[2], PB[3]], writes=[ETs[i][1] for i in range(4)])

    def mem_av(ci, hp, ob):
        nq = chunks[ci][1] * 128
        pso = bank(ob)[:, 0:nq]
        pss = bank(ob + 1)[:, 0:nq]
        for i in range(4):
            hl, mt = i // 2, i % 2
            et, eb = ETs[i]
            R.op("pe", lambda h, et=et, hl=hl, mt=mt, i=i: h.matmul(pso, VmP[:, mt, hp * 2 + hl, :], et[:, 0:nq], start=(i == 0), stop=(i == 3)),
                 reads=[Bvm, eb], writes=[PB[ob]], pe_accum=(i > 0))
        for i in range(4):
            hl = i // 2
            et, eb = ETs[i]
            R.op("pe", lambda h, et=et, hl=hl, i=i: h.matmul(pss, onesP[:, hl, :], et[:, 0:nq], start=(i == 0), stop=(i == 3)),
                 reads=[Bon, eb], writes=[PB[ob + 1]], pe_accum=(i > 0))

    def mem_tail(ci, hp, ob):
        qt0, nt = chunks[ci]
        nq = nt * 128
        q0 = qt0 * 128
        pso = bank(ob)[:, 0:nq]
        pss = bank(ob + 1)[:, 0:nq]
        tm, tmb = tmpring.next()
        R.op("dve", lambda h: h.reciprocal(tm[:, 0:nq], pss), reads=[PB[ob + 1]], writes=[tmb])
        for k in range(nt):
            R.op("dve", lambda h, k=k: h.tensor_tensor(yT[:, 6 + hp, 1 + q0 + k * 128:1 + q0 + (k + 1) * 128],
                                                       pso[:, k * 128:(k + 1) * 128], tm[:, k * 128:(k + 1) * 128], ALU.mult),
                 reads=[PB[ob], tmb], writes=[yTb[qt0 + k]])

    mblocks = [(ci, hp) for ci in range(len(chunks)) for hp in range(2)]
    mem_qk(*mblocks[0])
    mem_exp(*mblocks[0])
    for bi, (ci, hp) in enumerate(mblocks):
        ob = 4 + 2 * (bi % 2)
        mem_av(ci, hp, ob)
        if bi + 1 < len(mblocks):
            mem_qk(*mblocks[bi + 1])
            mem_exp(*mblocks[bi + 1])
        mem_tail(ci, hp, ob)

    SB = [[0, 1], [2, 3]]
    OB = [4, 5]
    UB = [6, 7]
    slopes = [2.0 ** (-2 * (i + 1)) for i in range(4)]

    def qk(ci, hh, kt, buf):
        qt0, nt = chunks[ci]
        nq = nt * 128
        q0 = qt0 * 128
        for m in range(2):
            b = SB[buf][m]
            ps = bank(b)[:, 0:nq]
            r0 = 64 * m
            R.op("pe", lambda h, ps=ps, r0=r0: h.matmul(ps, KT[r0:r0 + 64, hh, kt * 128:(kt + 1) * 128], QT[r0:r0 + 64, hh, q0:q0 + nq],
                                                        start=True, stop=False),
                 reads=[KTb[kt // 4], QTb[ci]], writes=[PB[b]])
            pk = PK[r0:r0 + 64, kt * 128:(kt + 1) * 128]
            if kt < qt0:
                R.op("pe", lambda h, ps=ps, pk=pk, r0=r0: h.matmul(ps, pk, PQL[r0:r0 + 64, q0:q0 + nq], start=False, stop=True),
                     reads=[Bpos], writes=[PB[b]], pe_accum=True)
            elif kt >= qt0 + nt:
                R.op("pe", lambda h, ps=ps, pk=pk, r0=r0: h.matmul(ps, pk, PQR[r0:r0 + 64, q0:q0 + nq], start=False, stop=True),
                     reads=[Bpos], writes=[PB[b]], pe_accum=True)
            else:
                i = kt - qt0
                if i > 0:
                    R.op("pe", lambda h, ps=ps, pk=pk, r0=r0, i=i: h.matmul(ps[:, 0:i * 128], pk, PQR[r0:r0 + 64, q0:q0 + i * 128], start=False, stop=False),
                         reads=[Bpos], writes=[PB[b]], pe_accum=True)
                if i < nt - 1:
                    R.op("pe", lambda h, ps=ps, pk=pk, r0=r0, i=i: h.matmul(ps[:, (i + 1) * 128:nq], pk, PQL[r0:r0 + 64, q0 + (i + 1) * 128:q0 + nq], start=False, stop=False),
                         reads=[Bpos], writes=[PB[b]], pe_accum=True)
                for a in range(2):
                    R.op("pe", lambda h, ps=ps, r0=r0, i=i, a=a: h.matmul(ps[:, i * 128:(i + 1) * 128], dg[r0:r0 + 64, a, :], dg[r0:r0 + 64, 2 + a, :],
                                                                          start=False, stop=(a == 1)),
                         reads=[Bpos], writes=[PB[b]], pe_accum=True)

    def expo(ci, hh, kt, buf):
        nq = chunks[ci][1] * 128
        et, eb = etring.next()
        b0 = SB[buf][0]
        if nq == 512:
            R.op("act", lambda h: h.activation(et[:, 0:1024], bank(b0, 2), AF.Exp, scale=slopes[hh]), reads=[PB[b0], PB[b0 + 1]], writes=[eb])
        else:
            R.op("act", lambda h: h.activation(et.rearrange("p (a b) -> p a b", a=2)[:, :, 0:nq], bank(b0, 2).rearrange("p (a b) -> p a b", a=2)[:, :, 0:nq],
                                               AF.Exp, scale=slopes[hh]),
                 reads=[PB[b0], PB[b0 + 1]], writes=[eb])
        return [(et[:, 0:512], eb), (et[:, 512:1024], eb)]

    def av_a(ci, hh, kt, ets):
        nq = chunks[ci][1] * 128
        for m in range(2):
            et, eb = ets[m]
            R.op("pe", lambda h, et=et, m=m: h.matmul(bank(OB[m])[:, 0:nq], Vt[:, kt, hh * 128:(hh + 1) * 128], et[:, 0:nq], start=(kt == 0), stop=(kt == NKT - 1)),
                 reads=[Vb[kt], eb], writes=[PB[OB[m]]], pe_accum=(kt > 0))

    def av_b(ci, hh, kp, e0, e1):
        nq = chunks[ci][1] * 128
        for j, (et, eb) in enumerate((e0[0], e0[1], e1[0], e1[1])):
            R.op("pe", lambda h, et=et, j=j: h.matmul(bank(UB[0])[32 * j:32 * j + 32, 0:nq], onesb[:, 0:32], et[:, 0:nq],
                                                      start=(kp == 0), stop=(kp == NKT // 2 - 1), tile_position=(0, 32 * j)),
                 reads=[Bc, eb], writes=[PB[UB[0]]], pe_accum=(kp > 0 or j > 0))

    def post_a(ci, hh):
        nq = chunks[ci][1] * 128
        o0, o0b = tmpring.next()
        o1, o1b = tmpring.next()
        r0, r0b = tmpring.next()
        r1, r1b = tmpring.next()
        y, yb = yacc[hh]
        sq, sqb = sqs[hh]
        R.op("act", lambda h: h.copy(o0[:, 0:nq], bank(OB[0])[:, 0:nq]), reads=[PB[OB[0]]], writes=[o0b])
        R.op("act", lambda h: h.copy(o1[:, 0:nq], bank(OB[1])[:, 0:nq]), reads=[PB[OB[1]]], writes=[o1b])
        R.op("dve", lambda h: h.tensor_copy(sq[:, 0:nq], bank(UB[0])[:, 0:nq]), reads=[PB[UB[0]]], writes=[sqb])
        R.op("pe", lambda h: h.matmul(bank(UB[0])[:, 0:nq], selm[:, 1, :], sq[:, 0:nq], start=True, stop=True), reads=[Bc, sqb], writes=[PB[UB[0]]])
        R.op("pe", lambda h: h.matmul(bank(UB[1])[:, 0:nq], selm[:, 0, :], sq[:, 0:nq], start=True, stop=True), reads=[Bc, sqb], writes=[PB[UB[1]]])
        R.op("dve", lambda h: h.reciprocal(r1[:, 0:nq], bank(UB[0])[:, 0:nq]), reads=[PB[UB[0]]], writes=[r1b])
        R.op("dve", lambda h: h.reciprocal(r0[:, 0:nq], bank(UB[1])[:, 0:nq]), reads=[PB[UB[1]]], writes=[r0b])
        R.op("dve", lambda h: h.tensor_tensor(y[:, 0:nq], o0[:, 0:nq], r0[:, 0:nq], ALU.mult), reads=[o0b, r0b], writes=[yb])
        R.op("dve", lambda h: h.tensor_tensor(r1[:, 0:nq], o1[:, 0:nq], r1[:, 0:nq], ALU.mult), reads=[o1b, r1b], writes=[r1b])
        R.op("dve", lambda h: h.scalar_tensor_tensor(y[:, 0:nq], r1[:, 0:nq], nlam, y[:, 0:nq], ALU.mult, ALU.add), reads=[r1b, yb, Bc], writes=[yb])
        R.op("pool", lambda h: h.tensor_tensor(sq[:, 0:nq], y[:, 0:nq], y[:, 0:nq], ALU.mult), reads=[yb], writes=[sqb])

    def post_b(ci):
        qt0, nt = chunks[ci]
        nq = nt * 128
        q0 = qt0 * 128
        for hh in range(4):
            sq, sqb = sqs[hh]
            R.op("pe", lambda h, hh=hh, sq=sq: h.matmul(bank(hh)[:, 0:nq], onesf, sq[:, 0:nq], start=True, stop=True), reads=[Bc, sqb], writes=[PB[hh]])
        for hh in range(4):
            sq, sqb = sqs[hh]
            R.op("act", lambda h, hh=hh, sq=sq: h.activation(sq[:, 0:nq], bank(hh)[:, 0:nq], AF.Sqrt, bias=epsc, scale=1.0 / 128.0), reads=[PB[hh], Bc], writes=[sqb])
        for hh in range(4):
            sq, sqb = sqs[hh]
            y, yb = yacc[hh]
            R.op("dve", lambda h, sq=sq: h.reciprocal(sq[:, 0:nq], sq[:, 0:nq]), reads=[sqb], writes=[sqb])
            for k in range(nt):
                R.op("dve", lambda h, k=k, hh=hh, sq=sq, y=y: h.scalar_tensor_tensor(yT[:, 2 + hh, 1 + q0 + k * 128:1 + q0 + (k + 1) * 128], y[:, k * 128:(k + 1) * 128], subg,
                                                                                  sq[:, k * 128:(k + 1) * 128], ALU.mult, ALU.mult),
                     reads=[yb, sqb, Bc], writes=[yTb[qt0 + k]])

    for ci in range(len(chunks)):
        for hh in range(4):
            qk(ci, hh, 0, 0)
            qk(ci, hh, 1, 1)
            for kp in range(NKT // 2):
                k0, k1 = 2 * kp, 2 * kp + 1
                e0 = expo(ci, hh, k0, 0)
                e1 = expo(ci, hh, k1, 1)
                if k1 + 1 < NKT:
                    qk(ci, hh, k0 + 2, 0)
                    qk(ci, hh, k1 + 2, 1)
                av_a(ci, hh, k0, e0)
                av_a(ci, hh, k1, e1)
                av_b(ci, hh, kp, e0, e1)
            post_a(ci, hh)
        post_b(ci)

    R.barrier()
    if DEBUG:
        R.dma(sp, dbg_d, yT, reads=yTb, writes=[Buf("dbg")])
        R.barrier()

    w_down = sb(0, (NJ, D), BF16)
    w_up = sb(45056, (8, 2 * DFF), BF16)
    Bwd = Buf("w_down")
    Bwu = [Buf("w_up%d" % c) for c in range(8)]
    o = P0
    w_out = sb(o, (8, D), BF16)
    o += 16384
    Bwo = Buf("w_out")
    vecs = sb(o, (4, D), F32)
    o += 16384
    Bvec = Buf("vecs")
    xa = [(sb(o + i * 4096, (D,), F32), Buf("xa%d" % i)) for i in range(4)]
    o += 16384
    ra = [(sb(o + i * 4096, (D,), F32), Buf("ra%d" % i)) for i in range(3)]
    o += 12288
    h1Tt = [(sb(o + i * 2048, (8, 128), BF16), Buf("h1Tt%d" % i)) for i in range(2)]
    o += 4096
    sts3 = []
    for i in range(8):
        sts3.append((sb(o, (12,), F32), sb(o + 48, (2,), F32), sb(o + 56, (1,), F32), sb(o + 60, (1,), F32), Buf("st3_%d" % i)))
        o += 64
    assert o <= CST, o

    for c in range(8):
        R.dma(sp, w_out[:, c, :], wout_b[c * 128:(c + 1) * 128, :], reads=[Bs_wout], writes=[Bwo])
    R.dma(sp, vecs.rearrange("p a b -> p (a b)"), vecs_d[0:4, :].rearrange("a b -> (a b)").partition_broadcast(128), writes=[Bvec])
    wq = []
    for j in range(NJ):
        wq.append(lambda j=j: R.dma(sp, w_down[:, j, :], wdn_b[j * 128:(j + 1) * 128, :], reads=[Bs_wdn], writes=[Bwd]))
    for c in range(4):
        for hf in range(2):
            wq.append(lambda c=c, hf=hf: R.dma(sp, w_up[:, c, hf * DFF:(hf + 1) * DFF], wup_b[c * 128:(c + 1) * 128, hf * DFF:(hf + 1) * DFF], reads=[Bs_wup[c]], writes=[Bwu[c]]))
    Bh1s = [Buf("h1s%d" % t) for t in range(NOWN)]
    Bh1T = [Buf("h1Ts%d" % t) for t in range(NOWN)]

    xaring = Ring(xa)
    raring = Ring(ra)
    st3ring = Ring(sts3)
    h1ring = Ring(h1Tt)
    zb = Ring([(0, 2), (2, 2)])
    tb3 = Ring([(4, 2), (6, 2)])

    def p3a_A1(t):
        xt, xb = xaring.next()
        R.dma(sp, xt, x_d[t * 128:(t + 1) * 128, :], writes=[xb])
        return (xt, xb, t)

    def rstd_chain(mv, rstd, nmr, tb):
        R.op("act", lambda h: h.activation(rstd, mv[:, 1:2], AF.Sqrt, bias=epsc, scale=1.0), reads=[tb, Bc], writes=[tb])
        R.op("dve", lambda h: h.reciprocal(rstd, rstd), reads=[tb], writes=[tb])
        R.op("dve", lambda h: h.scalar_tensor_tensor(nmr, mv[:, 0:1], -1.0, rstd, ALU.mult, ALU.mult), reads=[tb], writes=[tb])

    def p3a_A2(ctx):
        xt, xb, t = ctx
        R.op("act", lambda h: h.activation(xt, xt, AF.Identity, bias=pnm[:, t:t + 1], scale=prs[:, t:t + 1]), reads=[xb, Bprs], writes=[xb])
        R.op("pool", lambda h: h.tensor_tensor(xt, xt, vecs[:, 0, :], ALU.mult), reads=[xb, Bvec], writes=[xb])
        R.op("pool", lambda h: h.tensor_tensor(xt, xt, vecs[:, 1, :], ALU.add), reads=[xb, Bvec], writes=[xb])

    def p3a_Z(t):
        b0, _ = zb.next()
        psz = bank(b0, 2)
        for hf in range(2):
            for c in range(8):
                R.op("pe", lambda h, c=c, hf=hf: h.matmul(psz[:, hf * 512:(hf + 1) * 512], yT[:, c, 1 + t * 128:1 + (t + 1) * 128],
                                                          w_out[:, c, hf * 512:(hf + 1) * 512], start=(c == 0), stop=(c == 7)),
                     reads=[yTb[t], Bwo], writes=[PB[b0 + hf]], pe_accum=(c > 0))
        return b0, psz

    def p3a_C1(t, xt, xb, b0, psz):
        rt, rb = raring.next()
        R.op("dve", lambda h: h.scalar_tensor_tensor(rt, xt, ALPHA, psz, ALU.mult, ALU.add), reads=[xb, PB[b0], PB[b0 + 1]], writes=[rb])
        st, mv, rstd, nmr, tb = st3ring.next()
        for i in range(2):
            R.op("dve", lambda h, i=i: h.bn_stats(st[:, i * 6:(i + 1) * 6], rt[:, i * 512:(i + 1) * 512]), reads=[rb], writes=[tb])
        R.op("dve", lambda h: h.bn_aggr(mv, st[:, 0:12]), reads=[tb], writes=[tb])
        return (rt, rb, mv, rstd, nmr, tb)

    def p3a_C2a(ctx):
        rt, rb, mv, rstd, nmr, tb = ctx
        rstd_chain(mv, rstd, nmr, tb)
        R.op("act", lambda h: h.activation(rt, rt, AF.Identity, bias=nmr, scale=rstd), reads=[rb, tb], writes=[rb])

    def p3a_C2b(t, ctx):
        rt, rb, mv, rstd, nmr, tb = ctx
        c0, _ = tb3.next()
        pst_ = bank(c0, 2)
        for c in range(8):
            R.op("pe", lambda h, c=c: h.transpose(pst_[:, c * 128:(c + 1) * 128], rt[:, c * 128:(c + 1) * 128], identf),
                 reads=[rb, Bc], writes=[PB[c0 + c // 4]])
        ht, hb = h1ring.next()
        for c in range(8):
            R.op("act", lambda h, c=c: h.activation(ht[:, c, :], pst_[:, c * 128:(c + 1) * 128], AF.Identity, bias=lnp[:, 3, c:c + 1], scale=lnp[:, 2, c:c + 1]),
                 reads=[PB[c0 + c // 4], Bc], writes=[hb])
        R.dma(pool, h1T_d[:, :, 1 + t * 128:1 + (t + 1) * 128], ht, reads=[hb], writes=[Bh1T[t]])

    def p3a_C2c(t, ctx):
        rt, rb, mv, rstd, nmr, tb = ctx
        R.op("dve", lambda h: h.tensor_tensor(rt, rt, vecs[:, 2, :], ALU.mult), reads=[rb, Bvec], writes=[rb])
        R.op("dve", lambda h: h.tensor_tensor(rt, rt, vecs[:, 3, :], ALU.add), reads=[rb, Bvec], writes=[rb])
        R.dma(pool, h1s_d[t * 128:(t + 1) * 128, :], rt, reads=[rb], writes=[Bh1s[t]])

    aA = {}
    zZ = {}
    cC = {}
    for t in range(min(3, NOWN)):
        aA[t] = p3a_A1(t)
    p3a_A2(aA[0])
    p3a_A2(aA[1])
    zZ[0] = p3a_Z(0)
    zZ[1] = p3a_Z(1)
    cC[0] = p3a_C1(0, aA[0][0], aA[0][1], zZ[0][0], zZ[0][1])
    for t in range(NOWN):
        for _ in range(2):
            if wq:
                wq.pop(0)()
        p3a_C2a(cC[t])
        if t + 2 < NOWN:
            p3a_A2(aA[t + 2])
            zZ[t + 2] = p3a_Z(t + 2)
        p3a_C2b(t, cC[t])
        if t + 1 < NOWN:
            cC[t + 1] = p3a_C1(t + 1, aA[t + 1][0], aA[t + 1][1], zZ[t + 1][0], zZ[t + 1][1])
        if t + 3 < NOWN:
            aA[t + 3] = p3a_A1(t + 3)
        p3a_C2c(t, cC[t])
    while wq:
        wq.pop(0)()

    R.barrier()

    for c in range(4, 8):
        for hf in range(2):
            R.dma(sp, w_up[:, c, hf * DFF:(hf + 1) * DFF], wup_b[c * 128:(c + 1) * 128, hf * DFF:(hf + 1) * DFF], reads=[Bs_wup[c]], writes=[Bwu[c]])
    o = 135168
    GW = 258
    hg = [(sb(o + i * 4224, (8, GW), BF16), Buf("hg%d" % i)) for i in range(2)]
    o += 8448
    gTs = [(sb(o + i * 11264, (NJ, 256), BF16), [Buf("gT%d_%d" % (i, j)) for j in range(NJ)]) for i in range(2)]
    o += 22528
    h1r = [(sb(o + i * 4096, (D,), F32), Buf("h1r%d" % i)) for i in range(2)]
    o += 8192
    fo = [(sb(o + i * 4096, (D,), F32), Buf("fo%d" % i)) for i in range(2)]
    o += 8192
    vec2 = sb(o, (2, D), F32)
    o += 8192
    cw = sb(o, (2 * NJ, 4), F32)
    o += 704
    cts = []
    for i in range(4):
        cts.append((sb(o, (256,), F32), sb(o + 1024, (256,), F32), Buf("cg%d" % i), Buf("cv%d" % i)))
        o += 2048
    assert o <= CST - 256, o
    sts4 = sts3
    Bv2 = Buf("vec2")
    R.dma(sp, vec2.rearrange("p a b -> p (a b)"), vecs_d[4:6, :].rearrange("a b -> (a b)").partition_broadcast(128), writes=[Bv2])
    R.dma(sp, cw, cw_d, writes=[Bv2])
    o2 = o
    sts4 = []
    for i in range(4):
        sts4.append((sb(o2, (12,), F32), sb(o2 + 48, (2,), F32), sb(o2 + 56, (1,), F32), sb(o2 + 60, (1,), F32), Buf("st4_%d" % i)))
        o2 += 64
    assert o2 <= CST, o2

    hgring = Ring(hg)
    gTring = Ring(gTs)
    h1rring = Ring(h1r)
    foring = Ring(fo)
    ctring = Ring(cts)
    st4ring = Ring(sts4)
    abanks = Ring([(0, 1), (2, 3)])
    fbanks = Ring([(4, 2), (6, 2)])
    Bout = Buf("out")
    ngroups = NQ // 256

    def p3b_group(g):
        t0 = g * 256
        hgt, hgb = hgring.next()
        tl = [2 * g, 2 * g + 1, min(2 * g + 2, NOWN - 1)]
        if g == 0:
            R.dma(pool, hgt[:, :, 1:GW], h1T_d[:, :, 1:GW], reads=[Bh1T[i] for i in tl], writes=[hgb])
        else:
            R.dma(pool, hgt, h1T_d[:, :, t0:t0 + GW], reads=[Bh1T[i] for i in tl], writes=[hgb])
        if g == 0:
            R.op("dve", lambda h: h.memset(hgt[:, :, 0:1], 0.0), writes=[hgb])
        gT, gTb = gTring.next()
        tail = None
        for j in range(NJ):
            if j in (2, 5, 8, 11) and pending_ln2:
                pending_ln2.pop(0)()
            t2 = p3b_j(g, j, hgt, hgb, gT, gTb)
            if tail is not None:
                tail()
            tail = t2
        tail()
        p3b_down(g, gT, gTb)

    def p3b_j(g, j, hgt, hgb, gT, gTb):
        bg, bv = abanks.next()
        psg = bank(bg)[:, 0:GW]
        psv = bank(bv)[:, 0:GW]
        for c in range(8):
            R.op("pe", lambda h, c=c: h.matmul(psg, w_up[:, c, j * 128:(j + 1) * 128], hgt[:, c, :], start=(c == 0), stop=(c == 7)),
                 reads=[Bwu[c], hgb], writes=[PB[bg]], pe_accum=(c > 0), inc=(c == 7))
        for c in range(8):
            R.op("pe", lambda h, c=c: h.matmul(psv, w_up[:, c, DFF + j * 128:DFF + (j + 1) * 128], hgt[:, c, :], start=(c == 0), stop=(c == 7)),
                 reads=[Bwu[c], hgb], writes=[PB[bv]], pe_accum=(c > 0), inc=(c == 7))
        cg, cv, cgb, cvb = ctring.next()
        for (ps_, dst, jj, pb, db) in ((psg, cg, j, bg, cgb), (psv, cv, NJ + j, bv, cvb)):
            R.op("act", lambda h, ps_=ps_, dst=dst, jj=jj: h.activation(dst, ps_[:, 0:256], AF.Identity, bias=cw[:, jj, 3:4], scale=cw[:, jj, 0:1]),
                 reads=[PB[pb], Bv2], writes=[db])
        for (ps_, dst, jj, pb, db) in ((psg, cg, j, bg, cgb), (psv, cv, NJ + j, bv, cvb)):
            R.op("dve", lambda h, ps_=ps_, dst=dst, jj=jj: h.scalar_tensor_tensor(dst, ps_[:, 1:257], cw[:, jj, 1:2], dst, ALU.mult, ALU.add),
                 reads=[PB[pb], Bv2, db], writes=[db])
            R.op("dve", lambda h, ps_=ps_, dst=dst, jj=jj: h.scalar_tensor_tensor(dst, ps_[:, 2:258], cw[:, jj, 2:3], dst, ALU.mult, ALU.add),
                 reads=[PB[pb], Bv2, db], writes=[db])

        def tail():
            R.op("act", lambda h: h.activation(cg, cg, AF.Gelu), reads=[cgb], writes=[cgb])
            R.op("pool", lambda h: h.tensor_tensor(gT[:, j, :], cg, cv, ALU.mult), reads=[cgb, cvb], writes=[gTb[j]])
        return tail

    def p3b_down(g, gT, gTb):
        for i in range(2):
            p3b_down_tile(g, i, gT, gTb)

    def p3b_down_tile(g, i, gT, gTb):
        if True:
            t = 2 * g + i
            b0, _ = fbanks.next()
            psf = bank(b0, 2)
            for hf in range(2):
                for j in range(NJ):
                    R.op("pe", lambda h, j=j, hf=hf, psf=psf, gT=gT, i=i: h.matmul(psf[:, hf * 512:(hf + 1) * 512], gT[:, j, i * 128:(i + 1) * 128],
                                                                                    w_down[:, j, hf * 512:(hf + 1) * 512], start=(j == 0), stop=(j == NJ - 1)),
                         reads=[gTb[j], Bwd], writes=[PB[b0 + hf]], pe_accum=(j > 0), inc=(j == NJ - 1))
            hr, hrb = h1rring.next()
            R.dma(pool, hr, h1s_d[t * 128:(t + 1) * 128, :], reads=[Bh1s[t]], writes=[hrb])
            parts = p3b_ln2(t, b0, psf, hr, hrb)
            parts[0]()
            pending_ln2.extend(parts[1:])

    def p3b_ln2(t, b0, psf, hr, hrb):
        ft, fb = foring.next()
        st, mv, rstd, nmr, tb = st4ring.next()

        def part_a():
            R.op("dve", lambda h: h.scalar_tensor_tensor(ft, hr, ALPHA, psf, ALU.mult, ALU.add), reads=[hrb, PB[b0], PB[b0 + 1]], writes=[fb])
            for i in range(2):
                R.op("dve", lambda h, i=i: h.bn_stats(st[:, i * 6:(i + 1) * 6], ft[:, i * 512:(i + 1) * 512]), reads=[fb], writes=[tb])
            R.op("dve", lambda h: h.bn_aggr(mv, st[:, 0:12]), reads=[tb], writes=[tb])

        def part_b():
            R.op("act", lambda h: h.activation(rstd, mv[:, 1:2], AF.Sqrt, bias=epsc, scale=1.0), reads=[tb, Bc], writes=[tb])
            R.op("dve", lambda h: h.reciprocal(rstd, rstd), reads=[tb], writes=[tb])
            R.op("dve", lambda h: h.scalar_tensor_tensor(nmr, mv[:, 0:1], -1.0, rstd, ALU.mult, ALU.mult), reads=[tb], writes=[tb])

        def part_c():
            R.op("pool", lambda h: h.tensor_scalar(ft, ft, rstd, nmr, ALU.mult, ALU.add), reads=[fb, tb], writes=[fb])
            R.op("pool", lambda h: h.tensor_tensor(ft, ft, vec2[:, 0, :], ALU.mult), reads=[fb, Bv2], writes=[fb])
            R.op("pool", lambda h: h.tensor_tensor(ft, ft, vec2[:, 1, :], ALU.add), reads=[fb, Bv2], writes=[fb])
            R.dma(sp, out_d[t * 128:(t + 1) * 128, :], ft, reads=[fb], writes=[Bout], key="d_out")
        return [part_a, part_b, part_c]

    pending_ln2 = []
    for g in range(ngroups):
        p3b_group(g)
    while pending_ln2:
        pending_ln2.pop(0)()

    R.finish()
    return nc


def _bf16(a):
    import ml_dtypes
    return np.asarray(a, dtype=np.float32).astype(ml_dtypes.bfloat16)


def _pos_consts():
    npq = NOWN * 128
    pos = np.zeros((128, T + 2 * npq), np.float32)
    kp = np.arange(T)
    qp = np.arange(npq)
    for base in (0, 64):
        pos[base + 0, 0:T] = kp // 128
        pos[base + 1, 0:T] = kp % 128
        pos[base + 2, 0:T] = 1.0
        pos[base + 3, 0:T] = 1.0
        L = np.stack([np.full(npq, 128.0), np.ones(npq), -128.0 * (qp // 128), -1.0 * (qp % 128)])
        pos[base:base + 4, T:T + npq] = L
        pos[base:base + 4, T + npq:T + 2 * npq] = -L
    dg = np.zeros((128, 4, 128), np.float32)
    p = np.arange(128)
    k = np.arange(128)
    for a in range(2):
        kk = (p % 64) + 64 * a
        dg[p, a, kk] = 1.0
        dg[:, 2 + a, :] = -np.abs(k[None, :] - kk[:, None])
    return _bf16(pos), _bf16(dg)


def _selc():
    sel = np.zeros((128, 2, 128), np.float32)
    sel[0, 0, :] = 1.0
    sel[64, 0, :] = 1.0
    sel[32, 1, :] = 1.0
    sel[96, 1, :] = 1.0
    return sel


_NC_CACHE = {}


def kernel(x, mem, ln_emb_g, ln_emb_b, w_in, gmlp_ln_g, gmlp_ln_b, gmlp_ws, gmlp_bs,
           lambda_q1, lambda_k1, lambda_q2, lambda_k2, da_subln_g, mem_ln_g, mem_ln_b,
           w_mem_kv, w_out, ln1_g, ln1_b, w_up, conv_w, conv_b, w_down, ln2_g, ln2_b):
    f = lambda a: np.ascontiguousarray(np.asarray(a, dtype=np.float32))
    x = f(x); mem = f(mem)
    if "nc" not in _NC_CACHE:
        _NC_CACHE["nc"] = build_program()
    nc = _NC_CACHE["nc"]
    pos, dg = _pos_consts()
    fm = lambda v: f(v).reshape(8, 128).T
    lnp = np.stack([fm(ln_emb_g), fm(ln_emb_b), fm(ln1_g[0]), fm(ln1_b[0]), fm(mem_ln_g[0]), fm(mem_ln_b[0])], axis=1)
    vecs = np.stack([f(ln_emb_g), f(ln_emb_b), f(ln1_g[0]), f(ln1_b[0]), f(ln2_g[0]), f(ln2_b[0]), f(ln2_g[0]), f(ln2_b[0])])
    gml = np.stack([f(gmlp_ln_g[0]), f(gmlp_ln_b[0])])
    lam = np.stack([f(lambda_q1[0]), f(lambda_k1[0]), f(lambda_q2[0]), f(lambda_k2[0])])
    subg = f(da_subln_g[0]).reshape(128, 1)
    shared = {
        "w_in": f(w_in[0]), "w_mem_kv": f(w_mem_kv[0]), "w_out": f(w_out[0]), "w_up": f(w_up[0]), "w_down": f(w_down[0]),
        "lnp": np.ascontiguousarray(lnp), "vecs": np.ascontiguousarray(vecs), "gml": gml, "lam": lam, "subg": subg,
        "pos": pos, "dg": dg, "identf": np.eye(128, dtype=np.float32), "selc": _selc(),
    }
    ws = f(gmlp_ws[0]); bs = f(gmlp_bs[0]); cwt = f(conv_w[0]); cb = f(conv_b[0])
    in_maps = []
    for c in range(8):
        b, half = c // 2, c % 2
        if half == 0:
            xl = x[b]; wsl = ws; bsl = bs; cwl = cwt
        else:
            xl = x[b][::-1]; wsl = ws[:, ::-1, ::-1]; bsl = bs[:, ::-1]; cwl = cwt[::-1]
        cw4 = np.concatenate([cwl, cb[None, :]], axis=0)
        m = dict(shared)
        m["x"] = np.ascontiguousarray(xl)
        m["mem"] = mem[b]
        m["wsT"] = np.ascontiguousarray(wsl.transpose(2, 0, 1))
        m["bsT"] = np.ascontiguousarray(bsl.T)
        m["cw"] = np.ascontiguousarray(cw4.T.reshape(2 * NJ, 128, 4).transpose(1, 0, 2))
        in_maps.append(m)
    res = run_bass_kernel_spmd(nc, in_maps, core_ids=list(range(8)))
    out = np.empty((4, T, D), np.float32)
    for c in range(8):
        b, half = c // 2, c % 2
        o = res.results[c]["out"]
        if half == 0:
            out[b, 0:NQ] = o
        else:
            out[b, NQ:T] = o[::-1]
    kernel.last_results = res.results
    return out
```
